# Optimizing a Trainium2 kernel written in Bass

```python
import jax, jax.numpy as jnp
from jax import lax
import numpy as np

D_MODEL = 2048
BATCH = 1
SEQ = 16384
DEPTH = 2
DEC_BATCH = 8
DEC_SEQ = 32
PAST_LEN = 1024

CHUNK = 64
N_MIXERS = 2
N_CONV_LAYERS = (DEPTH + 1) // 2
N_POOL_LAYERS = DEPTH // 2
CONV_WIDTH = 3
CONV_HIST = CONV_WIDTH - 1
POOL_WINDOWS = (2, 4, 8, 16)
POOL_GROUPS = len(POOL_WINDOWS)
POOL_GC = D_MODEL // POOL_GROUPS
POOL_HIST = max(POOL_WINDOWS) - 1
D_FF = 4 * D_MODEL
EPS = 1e-6

kernel_name = "hybrid_shortconv_pool_stream_step"


def rmsnorm(x, g):
    xf = x.astype(jnp.float32)
    r = lax.rsqrt(jnp.mean(xf * xf, axis=-1, keepdims=True) + EPS)
    return (xf * r).astype(x.dtype) * g


def short_conv_mixer(z, hist, w_in, conv_w, w_out):
    L = z.shape[1]
    bch = jnp.einsum('bld,de->ble', z, w_in)
    b_gate, c_gate, h = jnp.split(bch, 3, axis=-1)
    u = c_gate * h
    u_ext = jnp.concatenate([hist.astype(u.dtype), u], axis=1)
    conv = u_ext[:, 0:L] * conv_w[0]
    for k in range(1, CONV_WIDTH):
        conv = conv + u_ext[:, k:k + L] * conv_w[k]
    y = jnp.einsum('bld,de->ble', b_gate * conv, w_out)
    return y, u_ext[:, -CONV_HIST:]


def pool_mixer(z, hist, pos0, pool_w, pool_scale):
    B, L, D = z.shape
    P = POOL_HIST
    z_ext = jnp.concatenate([hist.astype(z.dtype), z], axis=1)
    c = jnp.cumsum(z_ext.astype(jnp.float32), axis=1)
    c0 = jnp.concatenate([jnp.zeros((B, 1, D), jnp.float32), c], axis=1)
    pos = (pos0 + jnp.arange(L)).astype(jnp.float32)
    groups = []
    for g, w in enumerate(POOL_WINDOWS):
        lo, hi = g * POOL_GC, (g + 1) * POOL_GC
        s = c0[:, P + 1:P + 1 + L, lo:hi] - c0[:, P + 1 - w:P + 1 - w + L, lo:hi]
        cnt = jnp.minimum(pos + 1.0, float(w))[None, :, None]
        groups.append(s / cnt)
    pooled = jnp.concatenate(groups, axis=-1).astype(z.dtype) - z
    y = jnp.einsum('blgc,gce->blge', pooled.reshape(B, L, POOL_GROUPS, POOL_GC), pool_w)
    y = y.reshape(B, L, D) * pool_scale
    return y, z_ext[:, -P:]


def sqrelu_mlp(z, w1, w2):
    h = jax.nn.relu(jnp.einsum('bld,df->blf', z, w1))
    return jnp.einsum('blf,fd->bld', h * h, w2)


def trunk(x, conv_hist, pool_hist, pos0, mix_norm, ffn_norm, conv_w_in, conv_w, conv_w_out,
          pool_w, pool_scale, ffn_w1, ffn_w2, final_norm):
    new_conv, new_pool = [], []
    for i in range(DEPTH):
        z = rmsnorm(x, mix_norm[i])
        if i % N_MIXERS == 0:
            a = i // N_MIXERS
            y, st = short_conv_mixer(z, conv_hist[a], conv_w_in[a], conv_w[a], conv_w_out[a])
            new_conv.append(st)
        else:
            b = i // N_MIXERS
            y, st = pool_mixer(z, pool_hist[b], pos0, pool_w[b], pool_scale[b])
            new_pool.append(st)
        x = x + y
        x = x + sqrelu_mlp(rmsnorm(x, ffn_norm[i]), ffn_w1[i], ffn_w2[i])
    return rmsnorm(x, final_norm), jnp.stack(new_conv), jnp.stack(new_pool)


def setup_inputs(seed: int = 0) -> dict:
    key = jax.random.key(seed)
    ks = jax.random.split(key, 16)
    f32 = jnp.float32
    D = D_MODEL
    return {
        "x_prompt": jax.random.normal(ks[0], (BATCH, SEQ, D), f32),
        "x_sample": jax.random.normal(ks[1], (DEC_BATCH, DEC_SEQ, D), f32),
        "cache_conv": jax.random.normal(ks[2], (N_CONV_LAYERS, DEC_BATCH, CONV_HIST, D), f32),
        "cache_pool": jax.random.normal(ks[3], (N_POOL_LAYERS, DEC_BATCH, POOL_HIST, D), f32),
        "mix_norm": 1.0 + 0.02 * jax.random.normal(ks[4], (DEPTH, D), f32),
        "ffn_norm": 1.0 + 0.02 * jax.random.normal(ks[5], (DEPTH, D), f32),
        "conv_w_in": jax.random.normal(ks[6], (N_CONV_LAYERS, D, 3 * D), f32) * D ** -0.5,
        "conv_w": jax.random.normal(ks[7], (N_CONV_LAYERS, CONV_WIDTH, D), f32) * CONV_WIDTH ** -0.5,
        "conv_w_out": jax.random.normal(ks[8], (N_CONV_LAYERS, D, D), f32) * D ** -0.5,
        "pool_w": jax.random.normal(ks[9], (N_POOL_LAYERS, POOL_GROUPS, POOL_GC, POOL_GC), f32) * POOL_GC ** -0.5,
        "pool_scale": 1.0 + 0.02 * jax.random.normal(ks[10], (N_POOL_LAYERS, D), f32),
        "ffn_w1": jax.random.normal(ks[11], (DEPTH, D, D_FF), f32) * D ** -0.5,
        "ffn_w2": jax.random.normal(ks[12], (DEPTH, D_FF, D), f32) * D_FF ** -0.5,
        "final_norm": 1.0 + 0.02 * jax.random.normal(ks[13], (D,), f32),
    }


def reference(x_prompt, x_sample, cache_conv, cache_pool, mix_norm, ffn_norm, conv_w_in, conv_w,
              conv_w_out, pool_w, pool_scale, ffn_w1, ffn_w2, final_norm):
    zc = jnp.zeros((N_CONV_LAYERS, x_prompt.shape[0], CONV_HIST, D_MODEL), x_prompt.dtype)
    zp = jnp.zeros((N_POOL_LAYERS, x_prompt.shape[0], POOL_HIST, D_MODEL), x_prompt.dtype)
    y_prompt, conv_state_prompt, pool_state_prompt = trunk(
        x_prompt, zc, zp, 0, mix_norm, ffn_norm, conv_w_in, conv_w, conv_w_out,
        pool_w, pool_scale, ffn_w1, ffn_w2, final_norm)
    y_sample, conv_state_sample, pool_state_sample = trunk(
        x_sample, cache_conv, cache_pool, PAST_LEN, mix_norm, ffn_norm, conv_w_in, conv_w, conv_w_out,
        pool_w, pool_scale, ffn_w1, ffn_w2, final_norm)
    return (y_prompt, y_sample, conv_state_prompt, pool_state_prompt, conv_state_sample, pool_state_sample)
```

```python
import numpy as np
from contextlib import ExitStack
import concourse.bass as bass
import concourse.mybir as mybir
from concourse.bass_utils import run_bass_kernel_spmd

F32 = mybir.dt.float32
BF16 = mybir.dt.bfloat16
AF = mybir.ActivationFunctionType
ALU = mybir.AluOpType

HALO = 17
CONV_HIST = 2
POOL_HIST = 15
EPS = 1e-6
WINDOWS = (2, 4, 8, 16)
NVEC = 9


class Cfg:
    def __init__(self, D=2048, DFF=8192, NPC=2048, NS=32, SUBMAX=350, HG=8, NB=4,
                 NST=3, ncores=8):
        T = -(-(NPC + HALO + NS) // 2)
        self.D, self.DFF, self.NPC, self.NS, self.T = D, DFF, NPC, NS, T
        self.KC = D // 128
        self.HC = DFF // 128
        self.HG = HG
        self.NG = self.HC // HG
        self.PGC = self.KC // 4
        self.NB = NB
        self.NST = NST
        self.ncores = ncores
        self.NA = T - HALO
        self.NBP = NPC - self.NA
        self.SAMP0 = self.NBP
        assert self.NBP + NS <= T and self.NBP > 32
        nsub = -(-T // SUBMAX)
        base, rem = divmod(T, nsub)
        self.subs = []
        c = 0
        for i in range(nsub):
            n = base + (1 if i < rem else 0)
            self.subs.append((c, n))
            c += n
        assert max(n for _, n in self.subs) <= 512
        self.SC = 256
        self.SC2 = min(512, D)
        self.SLAB = max(self.KC * self.SC, HG * self.SC2, 2 * self.PGC * self.PGC * 128)
        self.NSCR = max(self.KC, 2 * HG)
        self.NOUT = NPC + NS + 2 * (CONV_HIST + POOL_HIST)
        self.MW = T + 32 + POOL_HIST


class Res:
    __slots__ = ("name", "wr", "rd")

    def __init__(self, name):
        self.name = name
        self.wr = {}
        self.rd = {}


class Sched:
    ENG = ("pe", "act", "dve", "pool", "sp")

    def __init__(self, nc, es):
        self.nc = nc
        self.es = es
        self.h = {}
        self.cnt = {}
        for e in self.ENG:
            self.h["e_" + e] = es.enter_context(nc.semaphore("s_" + e))
            self.cnt["e_" + e] = 0
        self.q = {e: [] for e in self.ENG}
        self.waited = {e: {} for e in self.ENG}

    def dma_sem(self, name):
        key = "d_" + name
        self.h[key] = self.es.enter_context(self.nc.semaphore(key))
        self.cnt[key] = 0
        return key

    @staticmethod
    def _flat(lst):
        out = []
        for r in lst:
            if isinstance(r, (list, tuple)):
                out.extend(Sched._flat(r))
            else:
                out.append(r)
        return out

    def _deps(self, eng, reads, writes, extra):
        d = {}
        for r in reads:
            for k, v in r.wr.items():
                if d.get(k, 0) < v:
                    d[k] = v
        for w in writes:
            for k, v in w.wr.items():
                if d.get(k, 0) < v:
                    d[k] = v
            for k, v in w.rd.items():
                if d.get(k, 0) < v:
                    d[k] = v
        for t in extra:
            if t is not None and d.get(t[0], 0) < t[1]:
                d[t[0]] = t[1]
        waits = []
        wd = self.waited[eng]
        for k, v in d.items():
            if eng == "pe" and k == "e_pe":
                continue
            if wd.get(k, 0) >= v:
                continue
            wd[k] = v
            waits.append((k, v))
        return waits

    @staticmethod
    def _commit(tok, reads, writes):
        k, v = tok
        for r in reads:
            if r.rd.get(k, 0) < v:
                r.rd[k] = v
        for w in writes:
            w.wr = {k: v}
            w.rd = {}

    def op(self, eng, fns, reads=(), writes=(), extra=()):
        if not isinstance(fns, (list, tuple)):
            fns = [fns]
        reads, writes = self._flat(reads), self._flat(writes)
        waits = self._deps(eng, reads, writes, extra)
        key = "e_" + eng
        self.cnt[key] += 1
        tok = (key, self.cnt[key])
        self.q[eng].append((waits, fns, key, 1))
        self._commit(tok, reads, writes)
        return tok

    def dma(self, eng, fn, dsem, reads=(), writes=(), extra=()):
        reads, writes = self._flat(reads), self._flat(writes)
        waits = self._deps(eng, reads, writes, extra)
        self.cnt[dsem] += 16
        tok = (dsem, self.cnt[dsem])
        self.q[eng].append((waits, [fn], dsem, 16))
        self._commit(tok, reads, writes)
        return tok

    def wait_only(self, eng, toks):
        waits = self._deps(eng, (), (), toks)
        self.q[eng].append((waits, [], None, 0))

    def replay(self, eng, e):
        for waits, fns, key, inc in self.q[eng]:
            for k, v in waits:
                e.wait_ge(self.h[k], v)
            n = len(fns)
            for i, fn in enumerate(fns):
                inst = fn(e)
                if i == n - 1 and key is not None:
                    inst.then_inc(self.h[key], inc)


def build_program(cfg):
    c = cfg
    D, DFF, KC, HC, HG, NG, PGC, T = c.D, c.DFF, c.KC, c.HC, c.HG, c.NG, c.PGC, c.T
    subs = c.subs
    NSUB = len(subs)
    nc = bass.Bass("TRN2", target_bir_lowering=False)

    def dram(name, shape, kind="ExternalInput"):
        return nc.dram_tensor(name, shape, F32, kind=kind).ap()

    xin = dram("xin", [2 * T + HALO, D])
    cvec = dram("cvec", [128, NVEC * KC])
    ident_d = dram("ident", [128, 128])
    posv_d = dram("posv", [128, POOL_HIST])
    w_in = dram("w_in", [D, 3 * D])
    w_out = dram("w_out", [D, D])
    pool_w = dram("pool_w", [4, PGC * 128, PGC * 128])
    w1 = dram("w1", [2, D, DFF])
    w2 = dram("w2", [2, DFF, D])
    out = dram("out", [c.NOUT, D], kind="ExternalOutput")

    with ExitStack() as es:
        def sb(name, shape, dt):
            return es.enter_context(nc.sbuf_tensor(name, shape, dt))

        x = sb("x", [128, KC, T], F32)
        z = sb("z", [128, KC, T], BF16)
        scr = sb("scr", [128, c.NSCR, T], BF16)
        slabs = [sb(f"slab{i}", [128, c.SLAB], BF16) for i in range(c.NB)]
        work = sb("work", [128, max(6 * c.MW, c.NST * D)], F32)
        mixb = [work[:, i * c.MW:(i + 1) * c.MW] for i in range(6)]
        stage = [work[:, i * D:(i + 1) * D] for i in range(c.NST)]
        rbuf = sb("rbuf", [128, T], F32)
        NSQ = 3
        SUBW = max(n for _, n in subs)
        sq = [sb(f"sq{i}", [128, SUBW], BF16) for i in range(NSQ)]
        NRT = 2
        rtb = [sb(f"rt{i}", [128, SUBW], F32) for i in range(NRT)]
        rs = sb("rs", [128, SUBW], F32)
        cv_sb = sb("cv_sb", [128, NVEC * KC], F32)
        ident = sb("ident_sb", [128, 128], F32)
        ones = sb("ones", [128, 128], BF16)
        posv = sb("posv_sb", [128, POOL_HIST], F32)
        tab = sb("tab", [128, 4, POOL_HIST], F32)
        pscw = sb("pscw", [128, KC], F32)
        hist = sb("hist", [128, KC, HALO], F32)
        uhp = sb("uhp", [128, KC, CONV_HIST], F32)
        zhp = sb("zhp", [128, KC, POOL_HIST], F32)
        NSTATE = 2 * (CONV_HIST + POOL_HIST)
        st_all = sb("st_all", [128, KC, NSTATE], F32)
        ps = es.enter_context(nc.psum_tensor("ps", [128, 8, 512], F32))

        S = Sched(nc, es)
        slab_sem = [S.dma_sem(f"slab{i}") for i in range(c.NB)]
        stage_sem = [S.dma_sem(f"stage{i}") for i in range(c.NST)]
        const_sem = [S.dma_sem(f"const{i}") for i in range(3)]

        X = [[Res(f"x{m}_{s}") for s in range(NSUB)] for m in range(KC)]
        Z = [[Res(f"z{m}_{s}") for s in range(NSUB)] for m in range(KC)]
        SCR = [[Res(f"scr{m}_{s}") for s in range(NSUB)] for m in range(c.NSCR)]
        SLABR = [Res(f"slab{i}") for i in range(c.NB)]
        WORKW = max(6 * c.MW, c.NST * D)
        bnds = sorted(set([i * c.MW for i in range(7)] + [i * D for i in range(c.NST + 1)] + [WORKW]))
        bnds = [b for b in bnds if b <= WORKW]
        ATOMS = [(bnds[i], bnds[i + 1], Res(f"atom{i}")) for i in range(len(bnds) - 1)]

        def atoms_in(lo, hi):
            return [r for (a, b, r) in ATOMS if a < hi and lo < b]
        MIX = [atoms_in(i * c.MW, (i + 1) * c.MW) for i in range(6)]
        NQ4 = KC // 4
        STAGE = [[Res(f"stage{i}_{q}") for q in range(NQ4)] for i in range(c.NST)]
        STX = [atoms_in(i * D, (i + 1) * D) for i in range(c.NST)]
        RB = [Res(f"rb{s}") for s in range(NSUB)]
        SQ = [Res(f"sq{i}") for i in range(NSQ)]
        RT = [Res(f"rt{i}") for i in range(NRT)]
        RS = Res("rs")
        BANK = [Res(f"bank{i}") for i in range(8)]
        CVEC = Res("cvec")
        IDENT = Res("ident")
        ONES = Res("ones")
        POSV = Res("posv")
        TAB = Res("tab")
        PSCW = Res("pscw")
        HIST = Res("hist")
        UHP = [Res(f"uhp{m}") for m in range(KC)]
        ZHP = [Res(f"zhp{m}") for m in range(KC)]
        STALL = Res("st_all")

        state = {"bank": 0, "slab": 0, "stage": 0, "sq": 0, "rt": 0, "alt": 0}

        reserved = set()

        def next_bank():
            while True:
                b = state["bank"]
                state["bank"] = (b + 1) % 8
                if b not in reserved:
                    return b

        def alt_eng():
            state["alt"] ^= 1
            return "act" if state["alt"] else "dve"

        def subs_overlapping(c0, n):
            return [s for s, (s0, sn) in enumerate(subs) if s0 < c0 + n and c0 < s0 + sn]

        def gv(i, m):
            return cv_sb[:, i * KC + m:i * KC + m + 1]

        def act_fn(out_, in_, func, scale=None, bias=None):
            kw = {}
            if scale is not None:
                kw["scale"] = scale
            if bias is not None:
                kw["bias"] = bias
            return lambda e: e.activation(out=out_, in_=in_, func=func, **kw)

        def copy_op(eng, out_, in_, reads, writes):
            if eng == "act":
                return S.op("act", act_fn(out_, in_, AF.Copy), reads, writes)
            return S.op("dve", lambda e: e.tensor_copy(out=out_, in_=in_), reads, writes)

        def tt(out_, in0, in1, op, reads, writes):
            return S.op("dve", lambda e: e.tensor_tensor(out=out_, in0=in0, in1=in1, op=op),
                        reads, writes)

        def stt(out_, in0, scalar, in1, op0, op1, reads, writes):
            return S.op("dve", lambda e: e.scalar_tensor_tensor(out=out_, in0=in0, scalar=scalar,
                                                               in1=in1, op0=op0, op1=op1),
                        reads, writes)

        def mm_fn(out_, lhsT, rhs, start, stop):
            return lambda e: e.matmul(out_, lhsT, rhs, start=start, stop=stop)

        def tr_fn(out_, in_, idn):
            return lambda e: e.transpose(out_, in_, idn)

        slab_gate = []

        def load_slab(src_ap, view_shape):
            i = state["slab"]
            state["slab"] = (i + 1) % c.NB
            k, n = view_shape
            view = slabs[i][:, 0:k * n].rearrange("p (k n) -> p k n", k=k)
            extra = [slab_gate.pop(0)] if slab_gate else []
            S.dma("pool", lambda e: e.dma_start(out=view, in_=src_ap), slab_sem[i],
                  reads=(), writes=[SLABR[i]], extra=extra)
            state["last_slab"] = i
            return view, SLABR[i]

        S.dma("sp", lambda e: e.dma_start(out=ident[:, :], in_=ident_d), const_sem[0], writes=[IDENT])
        S.dma("sp", lambda e: e.dma_start(out=cv_sb[:, :], in_=cvec), const_sem[1], writes=[CVEC])
        S.dma("sp", lambda e: e.dma_start(out=posv[:, :], in_=posv_d), const_sem[2], writes=[POSV])
        S.op("dve", lambda e: e.memset(ones[:, :], 1.0), (), [ONES])
        for i in range(6):
            S.op("dve", (lambda i: lambda e: e.memset(mixb[i][:, :], 0.0))(i), (), [MIX[i]])
        for g, w in enumerate(WINDOWS):
            S.op("dve", (lambda g, w: lambda e: e.tensor_scalar(
                out=tab[:, g, :], in0=posv[:, :], scalar1=float(w), scalar2=None, op0=ALU.min))(g, w),
                [POSV], [TAB])
        S.op("dve", lambda e: e.reciprocal(out=tab[:, :, :], in_=tab[:, :, :]), [TAB], [TAB])
        for g, w in enumerate(WINDOWS):
            S.op("dve", (lambda g, w: lambda e: e.tensor_scalar(
                out=tab[:, g, :], in0=tab[:, g, :], scalar1=float(w), scalar2=None, op0=ALU.mult))(g, w),
                [TAB], [TAB])
            S.op("dve", (lambda g, w: lambda e: e.tensor_scalar(
                out=pscw[:, g * PGC:(g + 1) * PGC], in0=cv_sb[:, 5 * KC + g * PGC:5 * KC + (g + 1) * PGC],
                scalar1=1.0 / w, scalar2=None, op0=ALU.mult))(g, w), [CVEC], [PSCW])

        def load_pass(p):
            ns0 = NormStream(zvec=0, zalt=True)
            ns0.begin()
            done_sub = 0
            load_toks = []
            for c0 in range(0, T, 128):
                n = min(128, T - c0)
                b = state["stage"]
                state["stage"] = (b + 1) % c.NST
                st = stage[b]
                r0 = p * T + c0
                ltok = S.dma("sp", (lambda st, r0, n: lambda e: e.dma_start(out=st[:n, :], in_=xin[r0:r0 + n, :]))(st, r0, n),
                             stage_sem[b], reads=(), writes=STAGE[b] + STX[b])
                load_toks.append(ltok)
                ss = subs_overlapping(c0, n)
                for m4 in range(NQ4):
                    bank = next_bank()
                    fns = [tr_fn(ps[:, bank, q * 128:q * 128 + n],
                                 st[:n, (4 * m4 + q) * 128:(4 * m4 + q + 1) * 128],
                                 ident[:n, :n]) for q in range(4)]
                    S.op("pe", fns, reads=STAGE[b] + STX[b] + [IDENT], writes=[BANK[bank]])
                    src = ps[:, bank, :].rearrange("p (q c) -> p q c", q=4)[:, :, :n]
                    dst = x[:, 4 * m4:4 * m4 + 4, c0:c0 + n]
                    copy_op(alt_eng(), dst, src, [BANK[bank]],
                            [X[4 * m4 + q][s] for q in range(4) for s in ss])
                while done_sub < NSUB and subs[done_sub][0] + subs[done_sub][1] <= c0 + n:
                    for m in range(KC):
                        ns0.tile_done(m, done_sub)
                    ns0.flush()
                    norm_sub(done_sub, 0, "zrc", ns0)
                    done_sub += 1
            nt = len(load_toks)
            slab_gate.clear()
            slab_gate.extend([load_toks[min(nt - 1, 4 + 2 * j)] for j in range(c.NB)])
            if p == 1:
                b = state["stage"]
                state["stage"] = (b + 1) % c.NST
                st = stage[b]
                S.dma("sp", (lambda st: lambda e: e.dma_start(out=st[:HALO, :], in_=xin[2 * T:2 * T + HALO, :]))(st),
                      stage_sem[b], reads=(), writes=STAGE[b] + STX[b])
                for m4 in range(NQ4):
                    bank = next_bank()
                    fns = [tr_fn(ps[:, bank, q * 128:q * 128 + HALO],
                                 st[:HALO, (4 * m4 + q) * 128:(4 * m4 + q + 1) * 128],
                                 ident[:HALO, :HALO]) for q in range(4)]
                    S.op("pe", fns, reads=STAGE[b] + STX[b] + [IDENT], writes=[BANK[bank]])
                    src_ = ps[:, bank, :].rearrange("p (q c) -> p q c", q=4)[:, :, :HALO]
                    copy_op(alt_eng(), hist[:, 4 * m4:4 * m4 + 4, :], src_, [BANK[bank]], [HIST])

        class NormStream:
            LAG = 2

            def __init__(self, zvec=None, zalt=False, xg=None):
                self.xg = xg
                self.banks = None
                self.pending = []
                self.zvec = zvec
                self.zalt = zalt

            def begin(self):
                self.banks = [next_bank() for _ in subs]
                reserved.update(self.banks)

            def _mm(self, m, s, qi):
                n = subs[s][1]
                S.op("pe", mm_fn(ps[:, self.banks[s], :n], ones[:, :], sq[qi][:, :n], m == 0, m == KC - 1),
                     [SQ[qi], ONES], [BANK[self.banks[s]]])

            def tile_done(self, m, s):
                c0, n = subs[s]
                qi = state["sq"]
                state["sq"] = (qi + 1) % NSQ
                if self.zalt and m % 2 == 0:
                    S.op("dve", (lambda m, c0, n, qi: lambda e: e.tensor_tensor(
                        out=sq[qi][:, :n], in0=x[:, m, c0:c0 + n], in1=x[:, m, c0:c0 + n], op=ALU.mult))(m, c0, n, qi),
                        [X[m][s]], [SQ[qi]])
                else:
                    S.op("act", act_fn(sq[qi][:, :n], x[:, m, c0:c0 + n], AF.Square), [X[m][s]], [SQ[qi]])
                if self.zvec is not None:
                    if self.zalt and m % 2 == 1:
                        S.op("dve", (lambda m, c0, n: lambda e: e.tensor_scalar(
                            out=z[:, m, c0:c0 + n], in0=x[:, m, c0:c0 + n], scalar1=gv(self.zvec, m),
                            scalar2=None, op0=ALU.mult))(m, c0, n), [X[m][s], CVEC], [Z[m][s]])
                    else:
                        S.op("act", act_fn(z[:, m, c0:c0 + n], x[:, m, c0:c0 + n], AF.Copy,
                                           scale=gv(self.zvec, m)), [X[m][s], CVEC], [Z[m][s]])
                if self.xg is not None:
                    S.op("act", act_fn(x[:, m, c0:c0 + n], x[:, m, c0:c0 + n], AF.Copy, scale=gv(self.xg, m)),
                         [X[m][s], CVEC], [X[m][s]])
                self.pending.append((m, s, qi))
                while len(self.pending) > self.LAG:
                    self._mm(*self.pending.pop(0))

            def flush(self):
                while self.pending:
                    self._mm(*self.pending.pop(0))

        rsb = [(rtb[0], RT[0]), (rtb[1], RT[1]), (rs, RS)]
        assert NSUB <= 3

        def norm_sub(s, vec_idx, mode, ns):
            c0, n = subs[s]
            bank = ns.banks[s]
            rsx, RSX = rsb[s]
            if mode in ("zr", "zrc"):
                S.op("act", act_fn(rsx[:, :n], ps[:, bank, :n], AF.Copy, scale=1.0 / D, bias=EPS),
                     [BANK[bank]], [RSX])
            else:
                S.op("act", act_fn(rsx[:, :n], ps[:, bank, :n], AF.Sqrt, scale=1.0 / D, bias=EPS),
                     [BANK[bank]], [RSX])
            reserved.discard(bank)
            if mode == "finalr":
                return
            if mode == "zrc":
                S.op("dve", (lambda n, rsx: lambda e: e.reciprocal(out=rsx[:, :n], in_=rsx[:, :n]))(n, rsx),
                     [RSX], [RSX])
                S.op("act", act_fn(rbuf[:, c0:c0 + n], rsx[:, :n], AF.Sqrt), [RSX], [RB[s]])
                return
            S.op("dve", (lambda c0, n, rsx: lambda e: e.reciprocal(out=rbuf[:, c0:c0 + n], in_=rsx[:, :n]))(c0, n, rsx),
                 [RSX], [RB[s]])
            if mode in ("stats", "zr"):
                return
            for m in range(KC):
                if mode == "z":
                    stt(z[:, m, c0:c0 + n], x[:, m, c0:c0 + n], gv(vec_idx, m), rbuf[:, c0:c0 + n],
                        ALU.mult, ALU.mult, [X[m][s], RB[s], CVEC], [Z[m][s]])
                else:
                    stt(x[:, m, c0:c0 + n], x[:, m, c0:c0 + n], gv(vec_idx, m), rbuf[:, c0:c0 + n],
                        ALU.mult, ALU.mult, [X[m][s], RB[s], CVEC], [X[m][s]])

        def norm(vec_idx, mode, ns):
            ns.flush()
            for s in range(NSUB):
                norm_sub(s, vec_idx, mode, ns)

        def pieces(p, c0, n, gap):
            if p == 0:
                return [(c0, n, 0)]
            nb = c.NBP
            out_ = []
            if c0 < nb:
                out_.append((c0, min(n, nb - c0), 0))
            if c0 + n > nb:
                lo = max(c0, nb)
                out_.append((lo, c0 + n - lo, gap))
            return out_

        def e_copy(eng, out_, in_, reads, writes):
            return S.op(eng, lambda e: e.tensor_copy(out=out_, in_=in_), reads, writes)

        def mixer_conv(p, ns):
            csb, ub, cvb = mixb[0:2], mixb[2:4], mixb[4:6]
            CSB, UB, CVB = MIX[0:2], MIX[2:4], MIX[4:6]
            G = CONV_HIST
            TT = T + (G if p == 1 else 0)
            nb = c.NBP
            for mp in range(KC // 2):
                for kind in (1, 2, 0):
                    col0 = kind * D + mp * c.SC
                    view, SR = load_slab(w_in[:, col0:col0 + c.SC].rearrange("(k p) n -> p k n", p=128),
                                         (KC, c.SC))
                    for j in range(2):
                        m = 2 * mp + j
                        for s, (c0, n) in enumerate(subs):
                            bank = next_bank()
                            fns = [mm_fn(ps[:, bank, :n], view[:, k, j * 128:(j + 1) * 128],
                                         z[:, k, c0:c0 + n], k == 0, k == KC - 1) for k in range(KC)]
                            S.op("pe", fns, [SR] + [Z[k][s] for k in range(KC)], [BANK[bank]])
                            if kind == 1:
                                copy_op("act", csb[j][:, c0:c0 + n], ps[:, bank, :n], [BANK[bank]], [CSB[j]])
                                tt(csb[j][:, c0:c0 + n], csb[j][:, c0:c0 + n], rsb[s][0][:, :n], ALU.mult,
                                   [CSB[j], rsb[s][1]], [CSB[j]])
                            elif kind == 2:
                                for (pc, pn, sh) in pieces(p, c0, n, G):
                                    tt(ub[j][:, 2 + pc + sh:2 + pc + sh + pn], csb[j][:, pc:pc + pn],
                                       ps[:, bank, pc - c0:pc - c0 + pn], ALU.mult, [BANK[bank], CSB[j]], [UB[j]])
                            else:
                                for (pc, pn, sh) in pieces(p, c0, n, G):
                                    tt(scr[:, m, pc:pc + pn], ps[:, bank, pc - c0:pc - c0 + pn],
                                       cvb[j][:, pc + sh:pc + sh + pn], ALU.mult, [BANK[bank], CVB[j]], [SCR[m][s]])
                        if kind == 2:
                            u = ub[j]
                            if p == 1:
                                e_copy("dve", u[:, 0:G], uhp[:, m, :], [UHP[m]], [UB[j]])
                                e_copy("dve", u[:, 2 + nb:2 + nb + G], hist[:, m, 0:G], [HIST], [UB[j]])
                            S.op("act", act_fn(cvb[j][:, 0:TT], u[:, 2:2 + TT], AF.Copy, scale=gv(8, m)),
                                 [UB[j], CVEC], [CVB[j]])
                            stt(cvb[j][:, 0:TT], u[:, 1:1 + TT], gv(7, m), cvb[j][:, 0:TT], ALU.mult, ALU.add,
                                [UB[j], CVB[j], CVEC], [CVB[j]])
                            stt(cvb[j][:, 0:TT], u[:, 0:TT], gv(6, m), cvb[j][:, 0:TT], ALU.mult, ALU.add,
                                [UB[j], CVB[j], CVEC], [CVB[j]])
                            for (pc, pn, sh) in pieces(p, 0, T, G):
                                ssx = subs_overlapping(pc, pn)
                                tt(cvb[j][:, pc + sh:pc + sh + pn], cvb[j][:, pc + sh:pc + sh + pn],
                                   rbuf[:, pc:pc + pn], ALU.mult, [CVB[j]] + [RB[s_] for s_ in ssx], [CVB[j]])
                            if p == 0:
                                S.op("act", act_fn(uhp[:, m, :], u[:, 2 + T - G:2 + T], AF.Copy), [UB[j]], [UHP[m]])
                            else:
                                S.op("act", act_fn(st_all[:, m, 0:2], u[:, 2 + nb - G:2 + nb], AF.Copy),
                                     [UB[j]], [STALL])
                                e1 = 2 + nb + c.NS + G
                                S.op("act", act_fn(st_all[:, m, 2:4], u[:, e1 - G:e1], AF.Copy),
                                     [UB[j]], [STALL])
            ns.begin()
            for q in range(D // c.SC):
                view, SR = load_slab(w_out[:, q * c.SC:(q + 1) * c.SC].rearrange("(k p) n -> p k n", p=128),
                                     (KC, c.SC))
                for j in range(c.SC // 128):
                    m = q * (c.SC // 128) + j
                    for s, (c0, n) in enumerate(subs):
                        bank = next_bank()
                        fns = [mm_fn(ps[:, bank, :n], view[:, k, j * 128:(j + 1) * 128],
                                     scr[:, k, c0:c0 + n], k == 0, k == KC - 1) for k in range(KC)]
                        S.op("pe", fns, [SR] + [SCR[k][s] for k in range(KC)], [BANK[bank]])
                        tt(x[:, m, c0:c0 + n], ps[:, bank, :n], x[:, m, c0:c0 + n], ALU.add,
                           [BANK[bank], X[m][s]], [X[m][s]])
                        ns.tile_done(m, s)

        def mixer_pool(p, ns):
            zbs, pa, pb = mixb[0:2], mixb[2:4], mixb[4:6]
            ZB, PA, PB = MIX[0:2], MIX[2:4], MIX[4:6]
            O = 32
            G = POOL_HIST
            TT = T + (G if p == 1 else 0)
            nb = c.NBP
            GCW = PGC * 128
            pw = []
            for g in range(4):
                view, SR = load_slab(pool_w[g].rearrange("(k p) n -> p k n", p=128), (PGC, GCW))
                i_ = state["last_slab"]
                view2 = slabs[i_][:, PGC * GCW:2 * PGC * GCW].rearrange("p (k n) -> p k n", k=PGC)
                pw.append((view, SR, view2))
            for i in range(2):
                S.op("dve", (lambda i: lambda e: e.memset(zbs[i][:, 0:O], 0.0))(i), (), [ZB[i]])
            def chunk_ew(m):
                g = m // PGC
                w = WINDOWS[g]
                i = m % 2
                zb = zbs[i]
                for (pc, pn, sh) in pieces(p, 0, T, G):
                    ss = subs_overlapping(pc, pn)
                    stt(zb[:, O + pc + sh:O + pc + sh + pn], x[:, m, pc:pc + pn], gv(1, m), rbuf[:, pc:pc + pn],
                        ALU.mult, ALU.mult, [X[m][s] for s in ss] + [RB[s] for s in ss] + [CVEC], [ZB[i]])
                if p == 0:
                    S.op("act", act_fn(zhp[:, m, :], zb[:, O + T - G:O + T], AF.Copy), [ZB[i]], [ZHP[m]])
                else:
                    e_copy("dve", zb[:, O - G:O], zhp[:, m, :], [ZHP[m]], [ZB[i]])
                    e_copy("dve", zb[:, O + nb:O + nb + G], hist[:, m, CONV_HIST:HALO], [HIST], [ZB[i]])
                    S.op("act", act_fn(st_all[:, m, 4:4 + G], zb[:, O + nb - G:O + nb], AF.Copy),
                         [ZB[i]], [STALL])
                    e1 = O + nb + c.NS + G
                    S.op("act", act_fn(st_all[:, m, 4 + G:4 + 2 * G], zb[:, e1 - G:e1], AF.Copy),
                         [ZB[i]], [STALL])
                cur, CUR = zb, ZB[i]
                if w == 2:
                    b0 = O - 14
                    ln = O + TT - b0
                    tt(pa[i][:, b0:b0 + ln], zb[:, b0:b0 + ln], zb[:, b0 - 1:b0 - 1 + ln], ALU.add, [ZB[i]], [PA[i]])
                else:
                    b0 = O - G
                    ln = O + TT - b0
                    S.op("dve", (lambda i, zb, b0, ln, w: lambda e: e.tensor_tensor_scan(
                        out=pa[i][:, b0:b0 + ln], data0=zb[:, b0:b0 + ln], data1=zb[:, b0 - w:b0 - w + ln],
                        initial=0.0, op0=ALU.add, op1=ALU.subtract))(i, zb, b0, ln, w), [ZB[i]], [PA[i]])
                if p == 0:
                    a0 = O + HALO
                    tt(pa[i][:, a0:a0 + G], pa[i][:, a0:a0 + G], tab[:, g, :], ALU.mult, [PA[i], TAB], [PA[i]])
                for (pc, pn, sh) in pieces(p, 0, T, G):
                    ssx = subs_overlapping(pc, pn)
                    S.op("act", act_fn(z[:, m, pc:pc + pn], pa[i][:, O + pc + sh:O + pc + sh + pn], AF.Copy),
                         [PA[i]], [Z[m][s] for s in ssx])
                    S.op("act", act_fn(scr[:, m, pc:pc + pn], zb[:, O + pc + sh:O + pc + sh + pn], AF.Copy),
                         [ZB[i]], [SCR[m][s] for s in ssx])
            ns.begin()
            for g in range(4):
                ns.zalt = (g == 3)
                for m_ in range(g * PGC, (g + 1) * PGC):
                    chunk_ew(m_)
                view, SR, view2 = pw[g]
                S.op("act", act_fn(view2, view, AF.Copy, scale=-float(WINDOWS[g])), [SR], [SR])
                for jo in range(PGC):
                    m = g * PGC + jo
                    for s, (c0, n) in enumerate(subs):
                        bank = next_bank()
                        fns = [mm_fn(ps[:, bank, :n], view[:, k, jo * 128:(jo + 1) * 128],
                                     z[:, g * PGC + k, c0:c0 + n], k == 0, False) for k in range(PGC)]
                        fns += [mm_fn(ps[:, bank, :n], view2[:, k, jo * 128:(jo + 1) * 128],
                                      scr[:, g * PGC + k, c0:c0 + n], False, k == PGC - 1) for k in range(PGC)]
                        S.op("pe", fns, [SR] + [Z[g * PGC + k][s] for k in range(PGC)]
                             + [SCR[g * PGC + k][s] for k in range(PGC)], [BANK[bank]])
                        stt(x[:, m, c0:c0 + n], ps[:, bank, :n], pscw[:, m:m + 1], x[:, m, c0:c0 + n],
                            ALU.mult, ALU.add, [BANK[bank], X[m][s], PSCW], [X[m][s]])
                for jo in range(PGC):
                    for s in range(NSUB):
                        ns.tile_done(g * PGC + jo, s)

        def ffn(l, ns):
            def w1_group(g, b):
                for q4 in range(HG // 2):
                    col0 = g * HG * 128 + q4 * c.SC
                    view, SR = load_slab(w1[l][:, col0:col0 + c.SC].rearrange("(k p) n -> p k n", p=128),
                                         (KC, c.SC))
                    for j in range(2):
                        hc = 2 * q4 + j
                        for s, (c0, n) in enumerate(subs):
                            bank = next_bank()
                            fns = [mm_fn(ps[:, bank, :n], view[:, k, j * 128:(j + 1) * 128],
                                         z[:, k, c0:c0 + n], k == 0, k == KC - 1) for k in range(KC)]
                            S.op("pe", fns, [SR] + [Z[k][s] for k in range(KC)], [BANK[bank]])
                            ri = state["rt"]
                            state["rt"] = (ri + 1) % NRT
                            S.op("act", act_fn(rtb[ri][:, :n], ps[:, bank, :n], AF.Relu), [BANK[bank]], [RT[ri]])
                            tt(rtb[ri][:, :n], rtb[ri][:, :n], rbuf[:, c0:c0 + n], ALU.mult,
                               [RT[ri], RB[s]], [RT[ri]])
                            tt(scr[:, b * HG + hc, c0:c0 + n], ps[:, bank, :n], rtb[ri][:, :n], ALU.mult,
                               [BANK[bank], RT[ri]], [SCR[b * HG + hc][s]])

            def w2_group(g, b, last=False):
                if last:
                    ns.begin()
                for q in range(D // c.SC2):
                    view, SR = load_slab(
                        w2[l][g * HG * 128:(g + 1) * HG * 128, q * c.SC2:(q + 1) * c.SC2].rearrange(
                            "(k p) n -> p k n", p=128), (HG, c.SC2))
                    for j in range(c.SC2 // 128):
                        m = q * (c.SC2 // 128) + j
                        for s, (c0, n) in enumerate(subs):
                            bank = next_bank()
                            fns = [mm_fn(ps[:, bank, :n], view[:, k, j * 128:(j + 1) * 128],
                                         scr[:, b * HG + k, c0:c0 + n], k == 0, k == HG - 1) for k in range(HG)]
                            S.op("pe", fns, [SR] + [SCR[b * HG + k][s] for k in range(HG)], [BANK[bank]])
                            tt(x[:, m, c0:c0 + n], ps[:, bank, :n], x[:, m, c0:c0 + n], ALU.add,
                               [BANK[bank], X[m][s]], [X[m][s]])
                            if last:
                                ns.tile_done(m, s)

            w1_group(0, 0)
            for g in range(1, NG):
                w1_group(g, g % 2)
                w2_group(g - 1, (g - 1) % 2)
            w2_group(NG - 1, (NG - 1) % 2, last=True)

        store_toks = []

        rtok = sb("rtok", [128, 4], F32)
        RTOK = [Res(f"rtok{i}") for i in range(4)]
        state["rtok"] = 0

        def out_tile(src_fn, src_res_fn, n, r0, rms=None):
            b = state["stage"]
            state["stage"] = (b + 1) % c.NST
            st = stage[b]
            ri = None
            if rms is not None:
                ri = state["rtok"]
                state["rtok"] = (ri + 1) % 4
                bank = next_bank()
                S.op("pe", tr_fn(ps[:n, bank, 0:128], rms[0], ident[:, :]), [IDENT, rms[1]], [BANK[bank]])
                S.op("dve", (lambda n, bank, ri: lambda e: e.reciprocal(out=rtok[:n, ri:ri + 1],
                                                                      in_=ps[:n, bank, 0:1]))(n, bank, ri),
                     [BANK[bank]], [RTOK[ri]])
            for m4 in range(NQ4):
                bank = next_bank()
                fns = [tr_fn(ps[:n, bank, q * 128:(q + 1) * 128], src_fn(4 * m4 + q), ident[:, :])
                       for q in range(4)]
                rr = [IDENT]
                for q in range(4):
                    rr += src_res_fn(4 * m4 + q)
                S.op("pe", fns, rr, [BANK[bank]])
                dst = st[:n, m4 * 512:(m4 + 1) * 512]
                wr = [STAGE[b][m4]] + (STX[b] if m4 == 0 else [])
                if ri is None:
                    copy_op(alt_eng(), dst, ps[:n, bank, :], [BANK[bank]], wr)
                elif alt_eng() == "act":
                    S.op("act", act_fn(dst, ps[:n, bank, :], AF.Copy, scale=rtok[:n, ri:ri + 1]),
                         [BANK[bank], RTOK[ri]], wr)
                else:
                    S.op("dve", (lambda dst, n, bank, ri: lambda e: e.tensor_scalar(
                        out=dst, in0=ps[:n, bank, :], scalar1=rtok[:n, ri:ri + 1], scalar2=None,
                        op0=ALU.mult))(dst, n, bank, ri), [BANK[bank], RTOK[ri]], wr)
            tok = S.dma("sp", (lambda st, n, r0: lambda e: e.dma_start(out=out[r0:r0 + n, :], in_=st[:n, :]))(st, n, r0),
                        stage_sem[b], reads=STAGE[b] + STX[b], writes=())
            store_toks.append(tok)

        def store_pass(p):
            if p == 0:
                segs = [(HALO, c.NA, 0)]
            else:
                segs = [(0, c.NBP, c.NA), (c.SAMP0, c.NS, c.NPC)]
            for (cs, cnt, r0) in segs:
                for s, (s0, sn) in enumerate(subs):
                    lo, hi = max(cs, s0), min(cs + cnt, s0 + sn)
                    for c0 in range(lo, hi, 128):
                        n = min(128, hi - c0)
                        rsx, RSX = rsb[s]
                        out_tile(lambda m, c0=c0, n=n: x[:, m, c0:c0 + n],
                                 lambda m, s=s: [X[m][s]], n, r0 + (c0 - cs),
                                 rms=(rsx[:, c0 - s0:c0 - s0 + n], RSX))

        for p in range(2):
            load_pass(p)
            ns = NormStream(zvec=2)
            mixer_conv(p, ns)
            norm(2, "zr", ns)
            ns = NormStream()
            ffn(0, ns)
            norm(1, "stats", ns)
            ns = NormStream(zvec=3)
            mixer_pool(p, ns)
            norm(3, "zr", ns)
            ns = NormStream(xg=4)
            ffn(1, ns)
            norm(4, "finalr", ns)
            store_pass(p)
        out_tile(lambda m: st_all[:, m, :], lambda m: [STALL], NSTATE, c.NPC + c.NS)
        S.wait_only("sp", store_toks)

        with nc.Block() as block:
            @block.tensor
            def _(e):
                S.replay("pe", e)

            @block.scalar
            def _(e):
                S.replay("act", e)

            @block.vector
            def _(e):
                S.replay("dve", e)

            @block.gpsimd
            def _(e):
                S.replay("pool", e)

            @block.sync
            def _(e):
                S.replay("sp", e)
    return nc


def host_inputs(cfg, core, x_prompt, x_sample, cache_conv, cache_pool, mix_norm, ffn_norm, conv_w_in,
                conv_w, conv_w_out, pool_w, pool_scale, ffn_w1, ffn_w2, final_norm):
    c = cfg
    T, D, KC = c.T, c.D, c.KC
    s0 = core * c.NPC
    xp = x_prompt[0]
    xin = np.zeros((2 * T + HALO, D), np.float32)
    lo = s0 - HALO
    a = max(lo, 0)
    xin[a - lo:HALO + c.NA] = xp[a:s0 + c.NA]
    xin[T:T + c.NBP] = xp[s0 + c.NA:s0 + c.NPC]
    xin[T + c.SAMP0:T + c.SAMP0 + c.NS] = x_sample[core]
    xin[2 * T:2 * T + CONV_HIST] = cache_conv[0, core]
    xin[2 * T + CONV_HIST:2 * T + HALO] = cache_pool[0, core]
    vecs = [mix_norm[0], mix_norm[1], ffn_norm[0], ffn_norm[1], final_norm, pool_scale[0],
            conv_w[0, 0], conv_w[0, 1], conv_w[0, 2]]
    cvec = np.concatenate([np.ascontiguousarray(v.reshape(KC, 128).T) for v in vecs], axis=1)
    posv = np.broadcast_to((s0 + 1 + np.arange(POOL_HIST)).astype(np.float32)[None, :], (128, POOL_HIST))
    return {
        "xin": xin,
        "cvec": np.ascontiguousarray(cvec, dtype=np.float32),
        "ident": np.eye(128, dtype=np.float32),
        "posv": np.ascontiguousarray(posv),
        "w_in": conv_w_in[0],
        "w_out": conv_w_out[0],
        "pool_w": pool_w[0],
        "w1": ffn_w1,
        "w2": ffn_w2,
    }


_NC_CACHE = {}


def run(cfg, inputs):
    key = (cfg.D, cfg.DFF, cfg.NPC, cfg.NS, cfg.T, cfg.subs[0][1])
    if key not in _NC_CACHE:
        _NC_CACHE[key] = build_program(cfg)
    nc = _NC_CACHE[key]
    inputs = {k: np.ascontiguousarray(np.asarray(v, dtype=np.float32)) for k, v in inputs.items()}
    in_maps = [host_inputs(cfg, core, **inputs) for core in range(cfg.ncores)]
    res = run_bass_kernel_spmd(nc, in_maps, core_ids=list(range(cfg.ncores)))
    outs = [r["out"] for r in res.results]
    c = cfg
    n = c.ncores
    y_prompt = np.concatenate([o[0:c.NPC] for o in outs], axis=0)[None]
    y_sample = np.stack([o[c.NPC:c.NPC + c.NS] for o in outs], axis=0)
    b = c.NPC + c.NS
    conv_p = outs[n - 1][b:b + 2][None, None]
    conv_s = np.stack([o[b + 2:b + 4] for o in outs], axis=0)[None]
    pool_p = outs[n - 1][b + 4:b + 19][None, None]
    pool_s = np.stack([o[b + 19:b + 34] for o in outs], axis=0)[None]
    return (np.ascontiguousarray(y_prompt), np.ascontiguousarray(y_sample), np.ascontiguousarray(conv_p),
            np.ascontiguousarray(pool_p), np.ascontiguousarray(conv_s), np.ascontiguousarray(pool_s))


def kernel(**inputs):
    return run(Cfg(), inputs)
```

```python
import numpy as np
from contextlib import ExitStack
import concourse.bass as bass
import concourse.mybir as mybir
from concourse.bass_utils import run_bass_kernel_spmd

F32 = mybir.dt.float32
BF16 = mybir.dt.bfloat16
AF = mybir.ActivationFunctionType
ALU = mybir.AluOpType

HALO = 17
CONV_HIST = 2
POOL_HIST = 15
EPS = 1e-6
WINDOWS = (2, 4, 8, 16)
NVEC = 9


class Cfg:
    def __init__(self, D=2048, DFF=8192, NPC=2048, NS=32, SUBMAX=350, HG=8, NB=4,
                 NST=3, ncores=8):
        T = -(-(NPC + HALO + NS) // 2)
        self.D, self.DFF, self.NPC, self.NS, self.T = D, DFF, NPC, NS, T
        self.KC = D // 128
        self.HC = DFF // 128
        self.HG = HG
        self.NG = self.HC // HG
        self.PGC = self.KC // 4
        self.NB = NB
        self.NST = NST
        self.ncores = ncores
        self.NA = T - HALO
        self.NBP = NPC - self.NA
        self.SAMP0 = self.NBP
        assert self.NBP + NS <= T and self.NBP > 32
        nsub = -(-T // SUBMAX)
        base, rem = divmod(T, nsub)
        self.subs = []
        c = 0
        for i in range(nsub):
            n = base + (1 if i < rem else 0)
            self.subs.append((c, n))
            c += n
        assert max(n for _, n in self.subs) <= 512
        self.SC = 256
        self.SC2 = min(512, D)
        self.SLAB = max(self.KC * self.SC, HG * self.SC2, self.PGC * self.PGC * 128)
        self.NSCR = max(self.KC, 2 * HG)
        self.NOUT = NPC + NS + 2 * (CONV_HIST + POOL_HIST)
        self.MW = T + 32 + POOL_HIST


class Res:
    __slots__ = ("name", "wr", "rd")

    def __init__(self, name):
        self.name = name
        self.wr = {}
        self.rd = {}


class Sched:
    ENG = ("pe", "act", "dve", "pool", "sp")

    def __init__(self, nc, es):
        self.nc = nc
        self.es = es
        self.h = {}
        self.cnt = {}
        for e in self.ENG:
            self.h["e_" + e] = es.enter_context(nc.semaphore("s_" + e))
            self.cnt["e_" + e] = 0
        self.q = {e: [] for e in self.ENG}
        self.waited = {e: {} for e in self.ENG}

    def dma_sem(self, name):
        key = "d_" + name
        self.h[key] = self.es.enter_context(self.nc.semaphore(key))
        self.cnt[key] = 0
        return key

    @staticmethod
    def _flat(lst):
        out = []
        for r in lst:
            if isinstance(r, (list, tuple)):
                out.extend(Sched._flat(r))
            else:
                out.append(r)
        return out

    def _deps(self, eng, reads, writes, extra):
        d = {}
        for r in reads:
            for k, v in r.wr.items():
                if d.get(k, 0) < v:
                    d[k] = v
        for w in writes:
            for k, v in w.wr.items():
                if d.get(k, 0) < v:
                    d[k] = v
            for k, v in w.rd.items():
                if d.get(k, 0) < v:
                    d[k] = v
        for t in extra:
            if t is not None and d.get(t[0], 0) < t[1]:
                d[t[0]] = t[1]
        waits = []
        wd = self.waited[eng]
        for k, v in d.items():
            if eng == "pe" and k == "e_pe":
                continue
            if wd.get(k, 0) >= v:
                continue
            wd[k] = v
            waits.append((k, v))
        return waits

    @staticmethod
    def _commit(tok, reads, writes):
        k, v = tok
        for r in reads:
            if r.rd.get(k, 0) < v:
                r.rd[k] = v
        for w in writes:
            w.wr = {k: v}
            w.rd = {}

    def op(self, eng, fns, reads=(), writes=(), extra=()):
        if not isinstance(fns, (list, tuple)):
            fns = [fns]
        reads, writes = self._flat(reads), self._flat(writes)
        waits = self._deps(eng, reads, writes, extra)
        key = "e_" + eng
        self.cnt[key] += 1
        tok = (key, self.cnt[key])
        self.q[eng].append((waits, fns, key, 1))
        self._commit(tok, reads, writes)
        return tok

    def dma(self, eng, fn, dsem, reads=(), writes=(), extra=()):
        reads, writes = self._flat(reads), self._flat(writes)
        waits = self._deps(eng, reads, writes, extra)
        self.cnt[dsem] += 16
        tok = (dsem, self.cnt[dsem])
        self.q[eng].append((waits, [fn], dsem, 16))
        self._commit(tok, reads, writes)
        return tok

    def wait_only(self, eng, toks):
        waits = self._deps(eng, (), (), toks)
        self.q[eng].append((waits, [], None, 0))

    def replay(self, eng, e):
        for waits, fns, key, inc in self.q[eng]:
            for k, v in waits:
                e.wait_ge(self.h[k], v)
            n = len(fns)
            for i, fn in enumerate(fns):
                inst = fn(e)
                if i == n - 1 and key is not None:
                    inst.then_inc(self.h[key], inc)


def build_program(cfg):
    c = cfg
    D, DFF, KC, HC, HG, NG, PGC, T = c.D, c.DFF, c.KC, c.HC, c.HG, c.NG, c.PGC, c.T
    subs = c.subs
    NSUB = len(subs)
    nc = bass.Bass("TRN2", target_bir_lowering=False)

    def dram(name, shape, kind="ExternalInput"):
        return nc.dram_tensor(name, shape, F32, kind=kind).ap()

    xin = dram("xin", [2 * T + HALO, D])
    cvec = dram("cvec", [128, NVEC * KC])
    ident_d = dram("ident", [128, 128])
    posv_d = dram("posv", [128, POOL_HIST])
    w_in = dram("w_in", [D, 3 * D])
    w_out = dram("w_out", [D, D])
    pool_w = dram("pool_w", [4, PGC * 128, PGC * 128])
    w1 = dram("w1", [2, D, DFF])
    w2 = dram("w2", [2, DFF, D])
    out = dram("out", [c.NOUT, D], kind="ExternalOutput")

    with ExitStack() as es:
        def sb(name, shape, dt):
            return es.enter_context(nc.sbuf_tensor(name, shape, dt))

        x = sb("x", [128, KC, T], F32)
        z = sb("z", [128, KC, T], BF16)
        scr = sb("scr", [128, c.NSCR, T], BF16)
        slabs = [sb(f"slab{i}", [128, c.SLAB], BF16) for i in range(c.NB)]
        work = sb("work", [128, max(6 * c.MW, c.NST * D)], F32)
        mixb = [work[:, i * c.MW:(i + 1) * c.MW] for i in range(6)]
        stage = [work[:, i * D:(i + 1) * D] for i in range(c.NST)]
        rbuf = sb("rbuf", [128, T], F32)
        NSQ = 3
        SUBW = max(n for _, n in subs)
        sq = [sb(f"sq{i}", [128, SUBW], BF16) for i in range(NSQ)]
        NRT = 2
        rtb = [sb(f"rt{i}", [128, SUBW], F32) for i in range(NRT)]
        rs = sb("rs", [128, SUBW], F32)
        cv_sb = sb("cv_sb", [128, NVEC * KC], F32)
        ident = sb("ident_sb", [128, 128], F32)
        ones = sb("ones", [128, 128], BF16)
        posv = sb("posv_sb", [128, POOL_HIST], F32)
        tab = sb("tab", [128, 4, POOL_HIST], F32)
        tmp15 = sb("tmp15", [128, POOL_HIST], F32)
        hist = sb("hist", [128, KC, HALO], F32)
        uhp = sb("uhp", [128, KC, CONV_HIST], F32)
        zhp = sb("zhp", [128, KC, POOL_HIST], F32)
        NSTATE = 2 * (CONV_HIST + POOL_HIST)
        st_all = sb("st_all", [128, KC, NSTATE], F32)
        ps = es.enter_context(nc.psum_tensor("ps", [128, 8, 512], F32))

        S = Sched(nc, es)
        slab_sem = [S.dma_sem(f"slab{i}") for i in range(c.NB)]
        stage_sem = [S.dma_sem(f"stage{i}") for i in range(c.NST)]
        const_sem = [S.dma_sem(f"const{i}") for i in range(3)]

        X = [[Res(f"x{m}_{s}") for s in range(NSUB)] for m in range(KC)]
        Z = [[Res(f"z{m}_{s}") for s in range(NSUB)] for m in range(KC)]
        SCR = [[Res(f"scr{m}_{s}") for s in range(NSUB)] for m in range(c.NSCR)]
        SLABR = [Res(f"slab{i}") for i in range(c.NB)]
        WORKW = max(6 * c.MW, c.NST * D)
        bnds = sorted(set([i * c.MW for i in range(7)] + [i * D for i in range(c.NST + 1)] + [WORKW]))
        bnds = [b for b in bnds if b <= WORKW]
        ATOMS = [(bnds[i], bnds[i + 1], Res(f"atom{i}")) for i in range(len(bnds) - 1)]

        def atoms_in(lo, hi):
            return [r for (a, b, r) in ATOMS if a < hi and lo < b]
        MIX = [atoms_in(i * c.MW, (i + 1) * c.MW) for i in range(6)]
        NQ4 = KC // 4
        STAGE = [[Res(f"stage{i}_{q}") for q in range(NQ4)] for i in range(c.NST)]
        STX = [atoms_in(i * D, (i + 1) * D) for i in range(c.NST)]
        RB = [Res(f"rb{s}") for s in range(NSUB)]
        SQ = [Res(f"sq{i}") for i in range(NSQ)]
        RT = [Res(f"rt{i}") for i in range(NRT)]
        RS = Res("rs")
        BANK = [Res(f"bank{i}") for i in range(8)]
        CVEC = Res("cvec")
        IDENT = Res("ident")
        ONES = Res("ones")
        POSV = Res("posv")
        TAB = Res("tab")
        TMP15 = Res("tmp15")
        HIST = Res("hist")
        UHP = [Res(f"uhp{m}") for m in range(KC)]
        ZHP = [Res(f"zhp{m}") for m in range(KC)]
        STALL = Res("st_all")

        state = {"bank": 0, "slab": 0, "stage": 0, "sq": 0, "rt": 0, "alt": 0}

        reserved = set()

        def next_bank():
            while True:
                b = state["bank"]
                state["bank"] = (b + 1) % 8
                if b not in reserved:
                    return b

        def alt_eng():
            state["alt"] ^= 1
            return "act" if state["alt"] else "dve"

        def subs_overlapping(c0, n):
            return [s for s, (s0, sn) in enumerate(subs) if s0 < c0 + n and c0 < s0 + sn]

        def gv(i, m):
            return cv_sb[:, i * KC + m:i * KC + m + 1]

        def act_fn(out_, in_, func, scale=None, bias=None):
            kw = {}
            if scale is not None:
                kw["scale"] = scale
            if bias is not None:
                kw["bias"] = bias
            return lambda e: e.activation(out=out_, in_=in_, func=func, **kw)

        def copy_op(eng, out_, in_, reads, writes):
            if eng == "act":
                return S.op("act", act_fn(out_, in_, AF.Copy), reads, writes)
            return S.op("dve", lambda e: e.tensor_copy(out=out_, in_=in_), reads, writes)

        def tt(out_, in0, in1, op, reads, writes):
            return S.op("dve", lambda e: e.tensor_tensor(out=out_, in0=in0, in1=in1, op=op),
                        reads, writes)

        def stt(out_, in0, scalar, in1, op0, op1, reads, writes):
            return S.op("dve", lambda e: e.scalar_tensor_tensor(out=out_, in0=in0, scalar=scalar,
                                                               in1=in1, op0=op0, op1=op1),
                        reads, writes)

        def mm_fn(out_, lhsT, rhs, start, stop):
            return lambda e: e.matmul(out_, lhsT, rhs, start=start, stop=stop)

        def tr_fn(out_, in_, idn):
            return lambda e: e.transpose(out_, in_, idn)

        slab_gate = []

        def load_slab(src_ap, view_shape):
            i = state["slab"]
            state["slab"] = (i + 1) % c.NB
            k, n = view_shape
            view = slabs[i][:, 0:k * n].rearrange("p (k n) -> p k n", k=k)
            extra = [slab_gate.pop(0)] if slab_gate else []
            S.dma("pool", lambda e: e.dma_start(out=view, in_=src_ap), slab_sem[i],
                  reads=(), writes=[SLABR[i]], extra=extra)
            return view, SLABR[i]

        S.dma("sp", lambda e: e.dma_start(out=ident[:, :], in_=ident_d), const_sem[0], writes=[IDENT])
        S.dma("sp", lambda e: e.dma_start(out=cv_sb[:, :], in_=cvec), const_sem[1], writes=[CVEC])
        S.dma("sp", lambda e: e.dma_start(out=posv[:, :], in_=posv_d), const_sem[2], writes=[POSV])
        S.op("dve", lambda e: e.memset(ones[:, :], 1.0), (), [ONES])
        for i in range(6):
            S.op("dve", (lambda i: lambda e: e.memset(mixb[i][:, :], 0.0))(i), (), [MIX[i]])
        for g, w in enumerate(WINDOWS):
            S.op("dve", (lambda g, w: lambda e: e.tensor_scalar(
                out=tab[:, g, :], in0=posv[:, :], scalar1=float(w), scalar2=None, op0=ALU.min))(g, w),
                [POSV], [TAB])
        S.op("dve", lambda e: e.reciprocal(out=tab[:, :, :], in_=tab[:, :, :]), [TAB], [TAB])

        def load_pass(p):
            ns0 = NormStream(zvec=0, zalt=True)
            ns0.begin()
            done_sub = 0
            load_toks = []
            for c0 in range(0, T, 128):
                n = min(128, T - c0)
                b = state["stage"]
                state["stage"] = (b + 1) % c.NST
                st = stage[b]
                r0 = p * T + c0
                ltok = S.dma("sp", (lambda st, r0, n: lambda e: e.dma_start(out=st[:n, :], in_=xin[r0:r0 + n, :]))(st, r0, n),
                             stage_sem[b], reads=(), writes=STAGE[b] + STX[b])
                load_toks.append(ltok)
                ss = subs_overlapping(c0, n)
                for m4 in range(NQ4):
                    bank = next_bank()
                    fns = [tr_fn(ps[:, bank, q * 128:q * 128 + n],
                                 st[:n, (4 * m4 + q) * 128:(4 * m4 + q + 1) * 128],
                                 ident[:n, :n]) for q in range(4)]
                    S.op("pe", fns, reads=STAGE[b] + STX[b] + [IDENT], writes=[BANK[bank]])
                    src = ps[:, bank, :].rearrange("p (q c) -> p q c", q=4)[:, :, :n]
                    dst = x[:, 4 * m4:4 * m4 + 4, c0:c0 + n]
                    copy_op(alt_eng(), dst, src, [BANK[bank]],
                            [X[4 * m4 + q][s] for q in range(4) for s in ss])
                while done_sub < NSUB and subs[done_sub][0] + subs[done_sub][1] <= c0 + n:
                    for m in range(KC):
                        ns0.tile_done(m, done_sub)
                    ns0.flush()
                    norm_sub(done_sub, 0, "zrc", ns0)
                    done_sub += 1
            nt = len(load_toks)
            slab_gate.clear()
            slab_gate.extend([load_toks[min(nt - 1, 4 + 2 * j)] for j in range(c.NB)])
            if p == 1:
                b = state["stage"]
                state["stage"] = (b + 1) % c.NST
                st = stage[b]
                S.dma("sp", (lambda st: lambda e: e.dma_start(out=st[:HALO, :], in_=xin[2 * T:2 * T + HALO, :]))(st),
                      stage_sem[b], reads=(), writes=STAGE[b] + STX[b])
                for m4 in range(NQ4):
                    bank = next_bank()
                    fns = [tr_fn(ps[:, bank, q * 128:q * 128 + HALO],
                                 st[:HALO, (4 * m4 + q) * 128:(4 * m4 + q + 1) * 128],
                                 ident[:HALO, :HALO]) for q in range(4)]
                    S.op("pe", fns, reads=STAGE[b] + STX[b] + [IDENT], writes=[BANK[bank]])
                    src_ = ps[:, bank, :].rearrange("p (q c) -> p q c", q=4)[:, :, :HALO]
                    copy_op(alt_eng(), hist[:, 4 * m4:4 * m4 + 4, :], src_, [BANK[bank]], [HIST])

        class NormStream:
            LAG = 2

            def __init__(self, zvec=None, zalt=False, xg=None):
                self.xg = xg
                self.banks = None
                self.pending = []
                self.zvec = zvec
                self.zalt = zalt

            def begin(self):
                self.banks = [next_bank() for _ in subs]
                reserved.update(self.banks)

            def _mm(self, m, s, qi):
                n = subs[s][1]
                S.op("pe", mm_fn(ps[:, self.banks[s], :n], ones[:, :], sq[qi][:, :n], m == 0, m == KC - 1),
                     [SQ[qi], ONES], [BANK[self.banks[s]]])

            def tile_done(self, m, s):
                c0, n = subs[s]
                qi = state["sq"]
                state["sq"] = (qi + 1) % NSQ
                if self.zalt and m % 2 == 0:
                    S.op("dve", (lambda m, c0, n, qi: lambda e: e.tensor_tensor(
                        out=sq[qi][:, :n], in0=x[:, m, c0:c0 + n], in1=x[:, m, c0:c0 + n], op=ALU.mult))(m, c0, n, qi),
                        [X[m][s]], [SQ[qi]])
                else:
                    S.op("act", act_fn(sq[qi][:, :n], x[:, m, c0:c0 + n], AF.Square), [X[m][s]], [SQ[qi]])
                if self.zvec is not None:
                    if self.zalt and m % 2 == 1:
                        S.op("dve", (lambda m, c0, n: lambda e: e.tensor_scalar(
                            out=z[:, m, c0:c0 + n], in0=x[:, m, c0:c0 + n], scalar1=gv(self.zvec, m),
                            scalar2=None, op0=ALU.mult))(m, c0, n), [X[m][s], CVEC], [Z[m][s]])
                    else:
                        S.op("act", act_fn(z[:, m, c0:c0 + n], x[:, m, c0:c0 + n], AF.Copy,
                                           scale=gv(self.zvec, m)), [X[m][s], CVEC], [Z[m][s]])
                if self.xg is not None:
                    S.op("act", act_fn(x[:, m, c0:c0 + n], x[:, m, c0:c0 + n], AF.Copy, scale=gv(self.xg, m)),
                         [X[m][s], CVEC], [X[m][s]])
                self.pending.append((m, s, qi))
                while len(self.pending) > self.LAG:
                    self._mm(*self.pending.pop(0))

            def flush(self):
                while self.pending:
                    self._mm(*self.pending.pop(0))

        rsb = [(rtb[0], RT[0]), (rtb[1], RT[1]), (rs, RS)]
        assert NSUB <= 3

        def norm_sub(s, vec_idx, mode, ns):
            c0, n = subs[s]
            bank = ns.banks[s]
            rsx, RSX = rsb[s]
            if mode in ("zr", "zrc"):
                S.op("act", act_fn(rsx[:, :n], ps[:, bank, :n], AF.Copy, scale=1.0 / D, bias=EPS),
                     [BANK[bank]], [RSX])
            else:
                S.op("act", act_fn(rsx[:, :n], ps[:, bank, :n], AF.Sqrt, scale=1.0 / D, bias=EPS),
                     [BANK[bank]], [RSX])
            reserved.discard(bank)
            if mode == "finalr":
                return
            if mode == "zrc":
                S.op("dve", (lambda n, rsx: lambda e: e.reciprocal(out=rsx[:, :n], in_=rsx[:, :n]))(n, rsx),
                     [RSX], [RSX])
                S.op("act", act_fn(rbuf[:, c0:c0 + n], rsx[:, :n], AF.Sqrt), [RSX], [RB[s]])
                return
            S.op("dve", (lambda c0, n, rsx: lambda e: e.reciprocal(out=rbuf[:, c0:c0 + n], in_=rsx[:, :n]))(c0, n, rsx),
                 [RSX], [RB[s]])
            if mode in ("stats", "zr"):
                return
            for m in range(KC):
                if mode == "z":
                    stt(z[:, m, c0:c0 + n], x[:, m, c0:c0 + n], gv(vec_idx, m), rbuf[:, c0:c0 + n],
                        ALU.mult, ALU.mult, [X[m][s], RB[s], CVEC], [Z[m][s]])
                else:
                    stt(x[:, m, c0:c0 + n], x[:, m, c0:c0 + n], gv(vec_idx, m), rbuf[:, c0:c0 + n],
                        ALU.mult, ALU.mult, [X[m][s], RB[s], CVEC], [X[m][s]])

        def norm(vec_idx, mode, ns):
            ns.flush()
            for s in range(NSUB):
                norm_sub(s, vec_idx, mode, ns)

        def pieces(p, c0, n, gap):
            if p == 0:
                return [(c0, n, 0)]
            nb = c.NBP
            out_ = []
            if c0 < nb:
                out_.append((c0, min(n, nb - c0), 0))
            if c0 + n > nb:
                lo = max(c0, nb)
                out_.append((lo, c0 + n - lo, gap))
            return out_

        def e_copy(eng, out_, in_, reads, writes):
            return S.op(eng, lambda e: e.tensor_copy(out=out_, in_=in_), reads, writes)

        def mixer_conv(p, ns):
            csb, ub, cvb = mixb[0:2], mixb[2:4], mixb[4:6]
            CSB, UB, CVB = MIX[0:2], MIX[2:4], MIX[4:6]
            G = CONV_HIST
            TT = T + (G if p == 1 else 0)
            nb = c.NBP
            for mp in range(KC // 2):
                for kind in (1, 2, 0):
                    col0 = kind * D + mp * c.SC
                    view, SR = load_slab(w_in[:, col0:col0 + c.SC].rearrange("(k p) n -> p k n", p=128),
                                         (KC, c.SC))
                    for j in range(2):
                        m = 2 * mp + j
                        for s, (c0, n) in enumerate(subs):
                            bank = next_bank()
                            fns = [mm_fn(ps[:, bank, :n], view[:, k, j * 128:(j + 1) * 128],
                                         z[:, k, c0:c0 + n], k == 0, k == KC - 1) for k in range(KC)]
                            S.op("pe", fns, [SR] + [Z[k][s] for k in range(KC)], [BANK[bank]])
                            if kind == 1:
                                copy_op("act", csb[j][:, c0:c0 + n], ps[:, bank, :n], [BANK[bank]], [CSB[j]])
                                tt(csb[j][:, c0:c0 + n], csb[j][:, c0:c0 + n], rsb[s][0][:, :n], ALU.mult,
                                   [CSB[j], rsb[s][1]], [CSB[j]])
                            elif kind == 2:
                                for (pc, pn, sh) in pieces(p, c0, n, G):
                                    tt(ub[j][:, 2 + pc + sh:2 + pc + sh + pn], csb[j][:, pc:pc + pn],
                                       ps[:, bank, pc - c0:pc - c0 + pn], ALU.mult, [BANK[bank], CSB[j]], [UB[j]])
                            else:
                                for (pc, pn, sh) in pieces(p, c0, n, G):
                                    tt(scr[:, m, pc:pc + pn], ps[:, bank, pc - c0:pc - c0 + pn],
                                       cvb[j][:, pc + sh:pc + sh + pn], ALU.mult, [BANK[bank], CVB[j]], [SCR[m][s]])
                        if kind == 2:
                            u = ub[j]
                            if p == 1:
                                e_copy("dve", u[:, 0:G], uhp[:, m, :], [UHP[m]], [UB[j]])
                                e_copy("dve", u[:, 2 + nb:2 + nb + G], hist[:, m, 0:G], [HIST], [UB[j]])
                            S.op("act", act_fn(cvb[j][:, 0:TT], u[:, 2:2 + TT], AF.Copy, scale=gv(8, m)),
                                 [UB[j], CVEC], [CVB[j]])
                            stt(cvb[j][:, 0:TT], u[:, 1:1 + TT], gv(7, m), cvb[j][:, 0:TT], ALU.mult, ALU.add,
                                [UB[j], CVB[j], CVEC], [CVB[j]])
                            stt(cvb[j][:, 0:TT], u[:, 0:TT], gv(6, m), cvb[j][:, 0:TT], ALU.mult, ALU.add,
                                [UB[j], CVB[j], CVEC], [CVB[j]])
                            for (pc, pn, sh) in pieces(p, 0, T, G):
                                ssx = subs_overlapping(pc, pn)
                                tt(cvb[j][:, pc + sh:pc + sh + pn], cvb[j][:, pc + sh:pc + sh + pn],
                                   rbuf[:, pc:pc + pn], ALU.mult, [CVB[j]] + [RB[s_] for s_ in ssx], [CVB[j]])
                            if p == 0:
                                S.op("act", act_fn(uhp[:, m, :], u[:, 2 + T - G:2 + T], AF.Copy), [UB[j]], [UHP[m]])
                            else:
                                S.op("act", act_fn(st_all[:, m, 0:2], u[:, 2 + nb - G:2 + nb], AF.Copy),
                                     [UB[j]], [STALL])
                                e1 = 2 + nb + c.NS + G
                                S.op("act", act_fn(st_all[:, m, 2:4], u[:, e1 - G:e1], AF.Copy),
                                     [UB[j]], [STALL])
            ns.begin()
            for q in range(D // c.SC):
                view, SR = load_slab(w_out[:, q * c.SC:(q + 1) * c.SC].rearrange("(k p) n -> p k n", p=128),
                                     (KC, c.SC))
                for j in range(c.SC // 128):
                    m = q * (c.SC // 128) + j
                    for s, (c0, n) in enumerate(subs):
                        bank = next_bank()
                        fns = [mm_fn(ps[:, bank, :n], view[:, k, j * 128:(j + 1) * 128],
                                     scr[:, k, c0:c0 + n], k == 0, k == KC - 1) for k in range(KC)]
                        S.op("pe", fns, [SR] + [SCR[k][s] for k in range(KC)], [BANK[bank]])
                        tt(x[:, m, c0:c0 + n], ps[:, bank, :n], x[:, m, c0:c0 + n], ALU.add,
                           [BANK[bank], X[m][s]], [X[m][s]])
                        ns.tile_done(m, s)

        def mixer_pool(p, ns):
            zbs, pa, pb = mixb[0:2], mixb[2:4], mixb[4:6]
            ZB, PA, PB = MIX[0:2], MIX[2:4], MIX[4:6]
            O = 32
            G = POOL_HIST
            TT = T + (G if p == 1 else 0)
            nb = c.NBP
            GCW = PGC * 128
            pw = [load_slab(pool_w[g].rearrange("(k p) n -> p k n", p=128), (PGC, GCW)) for g in range(4)]
            for i in range(2):
                S.op("dve", (lambda i: lambda e: e.memset(zbs[i][:, 0:O], 0.0))(i), (), [ZB[i]])
            def chunk_ew(m):
                g = m // PGC
                w = WINDOWS[g]
                i = m % 2
                zb = zbs[i]
                for (pc, pn, sh) in pieces(p, 0, T, G):
                    ss = subs_overlapping(pc, pn)
                    stt(zb[:, O + pc + sh:O + pc + sh + pn], x[:, m, pc:pc + pn], gv(1, m), rbuf[:, pc:pc + pn],
                        ALU.mult, ALU.mult, [X[m][s] for s in ss] + [RB[s] for s in ss] + [CVEC], [ZB[i]])
                if p == 0:
                    S.op("act", act_fn(zhp[:, m, :], zb[:, O + T - G:O + T], AF.Copy), [ZB[i]], [ZHP[m]])
                else:
                    S.op("act", act_fn(zb[:, O - G:O], zhp[:, m, :], AF.Copy), [ZHP[m]], [ZB[i]])
                    S.op("act", act_fn(zb[:, O + nb:O + nb + G], hist[:, m, CONV_HIST:HALO], AF.Copy),
                         [HIST], [ZB[i]])
                    S.op("act", act_fn(st_all[:, m, 4:4 + G], zb[:, O + nb - G:O + nb], AF.Copy),
                         [ZB[i]], [STALL])
                    e1 = O + nb + c.NS + G
                    S.op("act", act_fn(st_all[:, m, 4 + G:4 + 2 * G], zb[:, e1 - G:e1], AF.Copy),
                         [ZB[i]], [STALL])
                cur, CUR = zb, ZB[i]
                if w == 2:
                    b0 = O - 14
                    ln = O + TT - b0
                    tt(pa[i][:, b0:b0 + ln], zb[:, b0:b0 + ln], zb[:, b0 - 1:b0 - 1 + ln], ALU.add, [ZB[i]], [PA[i]])
                else:
                    b0 = O - G
                    ln = O + TT - b0
                    S.op("dve", (lambda i, zb, b0, ln, w: lambda e: e.tensor_tensor_scan(
                        out=pa[i][:, b0:b0 + ln], data0=zb[:, b0:b0 + ln], data1=zb[:, b0 - w:b0 - w + ln],
                        initial=0.0, op0=ALU.add, op1=ALU.subtract))(i, zb, b0, ln, w), [ZB[i]], [PA[i]])
                cur, CUR = pa[i], PA[i]
                for (pc, pn, sh) in pieces(p, 0, T, G):
                    stt(z[:, m, pc:pc + pn], cur[:, O + pc + sh:O + pc + sh + pn], 1.0 / w,
                        zb[:, O + pc + sh:O + pc + sh + pn], ALU.mult, ALU.subtract,
                        [CUR, ZB[i]], [Z[m][s] for s in subs_overlapping(pc, pn)])
                if p == 0:
                    a0 = O + HALO
                    tt(tmp15[:, :], cur[:, a0:a0 + G], tab[:, g, :], ALU.mult, [CUR, TAB], [TMP15])
                    tt(z[:, m, HALO:HALO + G], tmp15[:, :], zb[:, a0:a0 + G], ALU.subtract,
                       [TMP15, ZB[i]], [Z[m][s] for s in subs_overlapping(HALO, G)])
            ns.begin()
            for g in range(4):
                ns.zalt = (g == 3)
                for m_ in range(g * PGC, (g + 1) * PGC):
                    chunk_ew(m_)
                view, SR = pw[g]
                for jo in range(PGC):
                    m = g * PGC + jo
                    for s, (c0, n) in enumerate(subs):
                        bank = next_bank()
                        fns = [mm_fn(ps[:, bank, :n], view[:, k, jo * 128:(jo + 1) * 128],
                                     z[:, g * PGC + k, c0:c0 + n], k == 0, k == PGC - 1) for k in range(PGC)]
                        S.op("pe", fns, [SR] + [Z[g * PGC + k][s] for k in range(PGC)], [BANK[bank]])
                        stt(x[:, m, c0:c0 + n], ps[:, bank, :n], gv(5, m), x[:, m, c0:c0 + n],
                            ALU.mult, ALU.add, [BANK[bank], X[m][s], CVEC], [X[m][s]])
                for jo in range(PGC):
                    for s in range(NSUB):
                        ns.tile_done(g * PGC + jo, s)

        def ffn(l, ns):
            def w1_group(g, b):
                for q4 in range(HG // 2):
                    col0 = g * HG * 128 + q4 * c.SC
                    view, SR = load_slab(w1[l][:, col0:col0 + c.SC].rearrange("(k p) n -> p k n", p=128),
                                         (KC, c.SC))
                    for j in range(2):
                        hc = 2 * q4 + j
                        for s, (c0, n) in enumerate(subs):
                            bank = next_bank()
                            fns = [mm_fn(ps[:, bank, :n], view[:, k, j * 128:(j + 1) * 128],
                                         z[:, k, c0:c0 + n], k == 0, k == KC - 1) for k in range(KC)]
                            S.op("pe", fns, [SR] + [Z[k][s] for k in range(KC)], [BANK[bank]])
                            ri = state["rt"]
                            state["rt"] = (ri + 1) % NRT
                            S.op("act", act_fn(rtb[ri][:, :n], ps[:, bank, :n], AF.Relu), [BANK[bank]], [RT[ri]])
                            tt(rtb[ri][:, :n], rtb[ri][:, :n], rbuf[:, c0:c0 + n], ALU.mult,
                               [RT[ri], RB[s]], [RT[ri]])
                            tt(scr[:, b * HG + hc, c0:c0 + n], ps[:, bank, :n], rtb[ri][:, :n], ALU.mult,
                               [BANK[bank], RT[ri]], [SCR[b * HG + hc][s]])

            def w2_group(g, b, last=False):
                if last:
                    ns.begin()
                for q in range(D // c.SC2):
                    view, SR = load_slab(
                        w2[l][g * HG * 128:(g + 1) * HG * 128, q * c.SC2:(q + 1) * c.SC2].rearrange(
                            "(k p) n -> p k n", p=128), (HG, c.SC2))
                    for j in range(c.SC2 // 128):
                        m = q * (c.SC2 // 128) + j
                        for s, (c0, n) in enumerate(subs):
                            bank = next_bank()
                            fns = [mm_fn(ps[:, bank, :n], view[:, k, j * 128:(j + 1) * 128],
                                         scr[:, b * HG + k, c0:c0 + n], k == 0, k == HG - 1) for k in range(HG)]
                            S.op("pe", fns, [SR] + [SCR[b * HG + k][s] for k in range(HG)], [BANK[bank]])
                            tt(x[:, m, c0:c0 + n], ps[:, bank, :n], x[:, m, c0:c0 + n], ALU.add,
                               [BANK[bank], X[m][s]], [X[m][s]])
                            if last:
                                ns.tile_done(m, s)

            w1_group(0, 0)
            for g in range(1, NG):
                w1_group(g, g % 2)
                w2_group(g - 1, (g - 1) % 2)
            w2_group(NG - 1, (NG - 1) % 2, last=True)

        store_toks = []

        rtok = sb("rtok", [128, 4], F32)
        RTOK = [Res(f"rtok{i}") for i in range(4)]
        state["rtok"] = 0

        def out_tile(src_fn, src_res_fn, n, r0, rms=None):
            b = state["stage"]
            state["stage"] = (b + 1) % c.NST
            st = stage[b]
            ri = None
            if rms is not None:
                ri = state["rtok"]
                state["rtok"] = (ri + 1) % 4
                bank = next_bank()
                S.op("pe", tr_fn(ps[:n, bank, 0:128], rms[0], ident[:, :]), [IDENT, rms[1]], [BANK[bank]])
                S.op("dve", (lambda n, bank, ri: lambda e: e.reciprocal(out=rtok[:n, ri:ri + 1],
                                                                      in_=ps[:n, bank, 0:1]))(n, bank, ri),
                     [BANK[bank]], [RTOK[ri]])
            for m4 in range(NQ4):
                bank = next_bank()
                fns = [tr_fn(ps[:n, bank, q * 128:(q + 1) * 128], src_fn(4 * m4 + q), ident[:, :])
                       for q in range(4)]
                rr = [IDENT]
                for q in range(4):
                    rr += src_res_fn(4 * m4 + q)
                S.op("pe", fns, rr, [BANK[bank]])
                dst = st[:n, m4 * 512:(m4 + 1) * 512]
                wr = [STAGE[b][m4]] + (STX[b] if m4 == 0 else [])
                if ri is None:
                    copy_op(alt_eng(), dst, ps[:n, bank, :], [BANK[bank]], wr)
                elif alt_eng() == "act":
                    S.op("act", act_fn(dst, ps[:n, bank, :], AF.Copy, scale=rtok[:n, ri:ri + 1]),
                         [BANK[bank], RTOK[ri]], wr)
                else:
                    S.op("dve", (lambda dst, n, bank, ri: lambda e: e.tensor_scalar(
                        out=dst, in0=ps[:n, bank, :], scalar1=rtok[:n, ri:ri + 1], scalar2=None,
                        op0=ALU.mult))(dst, n, bank, ri), [BANK[bank], RTOK[ri]], wr)
            tok = S.dma("sp", (lambda st, n, r0: lambda e: e.dma_start(out=out[r0:r0 + n, :], in_=st[:n, :]))(st, n, r0),
                        stage_sem[b], reads=STAGE[b] + STX[b], writes=())
            store_toks.append(tok)

        def store_pass(p):
            if p == 0:
                segs = [(HALO, c.NA, 0)]
            else:
                segs = [(0, c.NBP, c.NA), (c.SAMP0, c.NS, c.NPC)]
            for (cs, cnt, r0) in segs:
                for s, (s0, sn) in enumerate(subs):
                    lo, hi = max(cs, s0), min(cs + cnt, s0 + sn)
                    for c0 in range(lo, hi, 128):
                        n = min(128, hi - c0)
                        rsx, RSX = rsb[s]
                        out_tile(lambda m, c0=c0, n=n: x[:, m, c0:c0 + n],
                                 lambda m, s=s: [X[m][s]], n, r0 + (c0 - cs),
                                 rms=(rsx[:, c0 - s0:c0 - s0 + n], RSX))

        for p in range(2):
            load_pass(p)
            ns = NormStream(zvec=2)
            mixer_conv(p, ns)
            norm(2, "zr", ns)
            ns = NormStream()
            ffn(0, ns)
            norm(1, "stats", ns)
            if p == 0:
                subs_full = list(subs)
                subs[0] = (HALO, subs[0][1] - HALO)
            ns = NormStream(zvec=3)
            mixer_pool(p, ns)
            norm(3, "zr", ns)
            ns = NormStream(xg=4)
            ffn(1, ns)
            norm(4, "finalr", ns)
            store_pass(p)
            if p == 0:
                subs[:] = subs_full
        out_tile(lambda m: st_all[:, m, :], lambda m: [STALL], NSTATE, c.NPC + c.NS)
        S.wait_only("sp", store_toks)

        with nc.Block() as block:
            @block.tensor
            def _(e):
                S.replay("pe", e)

            @block.scalar
            def _(e):
                S.replay("act", e)

            @block.vector
            def _(e):
                S.replay("dve", e)

            @block.gpsimd
            def _(e):
                S.replay("pool", e)

            @block.sync
            def _(e):
                S.replay("sp", e)
    return nc


def host_inputs(cfg, core, x_prompt, x_sample, cache_conv, cache_pool, mix_norm, ffn_norm, conv_w_in,
                conv_w, conv_w_out, pool_w, pool_scale, ffn_w1, ffn_w2, final_norm):
    c = cfg
    T, D, KC = c.T, c.D, c.KC
    s0 = core * c.NPC
    xp = x_prompt[0]
    xin = np.zeros((2 * T + HALO, D), np.float32)
    lo = s0 - HALO
    a = max(lo, 0)
    xin[a - lo:HALO + c.NA] = xp[a:s0 + c.NA]
    xin[T:T + c.NBP] = xp[s0 + c.NA:s0 + c.NPC]
    xin[T + c.SAMP0:T + c.SAMP0 + c.NS] = x_sample[core]
    xin[2 * T:2 * T + CONV_HIST] = cache_conv[0, core]
    xin[2 * T + CONV_HIST:2 * T + HALO] = cache_pool[0, core]
    vecs = [mix_norm[0], mix_norm[1], ffn_norm[0], ffn_norm[1], final_norm, pool_scale[0],
            conv_w[0, 0], conv_w[0, 1], conv_w[0, 2]]
    cvec = np.concatenate([np.ascontiguousarray(v.reshape(KC, 128).T) for v in vecs], axis=1)
    posv = np.broadcast_to((s0 + 1 + np.arange(POOL_HIST)).astype(np.float32)[None, :], (128, POOL_HIST))
    return {
        "xin": xin,
        "cvec": np.ascontiguousarray(cvec, dtype=np.float32),
        "ident": np.eye(128, dtype=np.float32),
        "posv": np.ascontiguousarray(posv),
        "w_in": conv_w_in[0],
        "w_out": conv_w_out[0],
        "pool_w": pool_w[0],
        "w1": ffn_w1,
        "w2": ffn_w2,
    }


_NC_CACHE = {}


def run(cfg, inputs):
    key = (cfg.D, cfg.DFF, cfg.NPC, cfg.NS, cfg.T, cfg.subs[0][1])
    if key not in _NC_CACHE:
        _NC_CACHE[key] = build_program(cfg)
    nc = _NC_CACHE[key]
    inputs = {k: np.ascontiguousarray(np.asarray(v, dtype=np.float32)) for k, v in inputs.items()}
    in_maps = [host_inputs(cfg, core, **inputs) for core in range(cfg.ncores)]
    res = run_bass_kernel_spmd(nc, in_maps, core_ids=list(range(cfg.ncores)))
    outs = [r["out"] for r in res.results]
    c = cfg
    n = c.ncores
    y_prompt = np.concatenate([o[0:c.NPC] for o in outs], axis=0)[None]
    y_sample = np.stack([o[c.NPC:c.NPC + c.NS] for o in outs], axis=0)
    b = c.NPC + c.NS
    conv_p = outs[n - 1][b:b + 2][None, None]
    conv_s = np.stack([o[b + 2:b + 4] for o in outs], axis=0)[None]
    pool_p = outs[n - 1][b + 4:b + 19][None, None]
    pool_s = np.stack([o[b + 19:b + 34] for o in outs], axis=0)[None]
    return (np.ascontiguousarray(y_prompt), np.ascontiguousarray(y_sample), np.ascontiguousarray(conv_p),
            np.ascontiguousarray(pool_p), np.ascontiguousarray(conv_s), np.ascontiguousarray(pool_s))


def kernel(**inputs):
    return run(Cfg(), inputs)
```

```python
import numpy as np
from contextlib import ExitStack
import concourse.bass as bass
import concourse.mybir as mybir
from concourse.bass_utils import run_bass_kernel_spmd

F32 = mybir.dt.float32
BF16 = mybir.dt.bfloat16
AF = mybir.ActivationFunctionType
ALU = mybir.AluOpType

HALO = 17
CONV_HIST = 2
POOL_HIST = 15
EPS = 1e-6
WINDOWS = (2, 4, 8, 16)
NVEC = 9


class Cfg:
    def __init__(self, D=2048, DFF=8192, NPC=2048, NS=32, SUBMAX=350, HG=8, NB=4,
                 NST=3, ncores=8):
        T = -(-(NPC + HALO + NS) // 2)
        self.D, self.DFF, self.NPC, self.NS, self.T = D, DFF, NPC, NS, T
        self.KC = D // 128
        self.HC = DFF // 128
        self.HG = HG
        self.NG = self.HC // HG
        self.PGC = self.KC // 4
        self.NB = NB
        self.NST = NST
        self.ncores = ncores
        self.NA = T - HALO
        self.NBP = NPC - self.NA
        self.SAMP0 = self.NBP
        assert self.NBP + NS <= T and self.NBP > 32
        nsub = -(-T // SUBMAX)
        base, rem = divmod(T, nsub)
        self.subs = []
        c = 0
        for i in range(nsub):
            n = base + (1 if i < rem else 0)
            self.subs.append((c, n))
            c += n
        assert max(n for _, n in self.subs) <= 512
        self.SC = 256
        self.SC2 = min(512, D)
        self.SLAB = max(self.KC * self.SC, HG * self.SC2, self.PGC * self.PGC * 128)
        self.NSCR = max(self.KC, 2 * HG)
        self.NOUT = NPC + NS + 2 * (CONV_HIST + POOL_HIST)
        self.MW = T + 32 + POOL_HIST


class Res:
    __slots__ = ("name", "wr", "rd")

    def __init__(self, name):
        self.name = name
        self.wr = {}
        self.rd = {}


class Sched:
    ENG = ("pe", "act", "dve", "pool", "sp")

    def __init__(self, nc, es):
        self.nc = nc
        self.es = es
        self.h = {}
        self.cnt = {}
        for e in self.ENG:
            self.h["e_" + e] = es.enter_context(nc.semaphore("s_" + e))
            self.cnt["e_" + e] = 0
        self.q = {e: [] for e in self.ENG}
        self.waited = {e: {} for e in self.ENG}

    def dma_sem(self, name):
        key = "d_" + name
        self.h[key] = self.es.enter_context(self.nc.semaphore(key))
        self.cnt[key] = 0
        return key

    @staticmethod
    def _flat(lst):
        out = []
        for r in lst:
            if isinstance(r, (list, tuple)):
                out.extend(Sched._flat(r))
            else:
                out.append(r)
        return out

    def _deps(self, eng, reads, writes, extra):
        d = {}
        for r in reads:
            for k, v in r.wr.items():
                if d.get(k, 0) < v:
                    d[k] = v
        for w in writes:
            for k, v in w.wr.items():
                if d.get(k, 0) < v:
                    d[k] = v
            for k, v in w.rd.items():
                if d.get(k, 0) < v:
                    d[k] = v
        for t in extra:
            if t is not None and d.get(t[0], 0) < t[1]:
                d[t[0]] = t[1]
        waits = []
        wd = self.waited[eng]
        for k, v in d.items():
            if eng == "pe" and k == "e_pe":
                continue
            if wd.get(k, 0) >= v:
                continue
            wd[k] = v
            waits.append((k, v))
        return waits

    @staticmethod
    def _commit(tok, reads, writes):
        k, v = tok
        for r in reads:
            if r.rd.get(k, 0) < v:
                r.rd[k] = v
        for w in writes:
            w.wr = {k: v}
            w.rd = {}

    def op(self, eng, fns, reads=(), writes=(), extra=()):
        if not isinstance(fns, (list, tuple)):
            fns = [fns]
        reads, writes = self._flat(reads), self._flat(writes)
        waits = self._deps(eng, reads, writes, extra)
        key = "e_" + eng
        self.cnt[key] += 1
        tok = (key, self.cnt[key])
        self.q[eng].append((waits, fns, key, 1))
        self._commit(tok, reads, writes)
        return tok

    def dma(self, eng, fn, dsem, reads=(), writes=(), extra=()):
        reads, writes = self._flat(reads), self._flat(writes)
        waits = self._deps(eng, reads, writes, extra)
        self.cnt[dsem] += 16
        tok = (dsem, self.cnt[dsem])
        self.q[eng].append((waits, [fn], dsem, 16))
        self._commit(tok, reads, writes)
        return tok

    def wait_only(self, eng, toks):
        waits = self._deps(eng, (), (), toks)
        self.q[eng].append((waits, [], None, 0))

    def replay(self, eng, e):
        for waits, fns, key, inc in self.q[eng]:
            for k, v in waits:
                e.wait_ge(self.h[k], v)
            n = len(fns)
            for i, fn in enumerate(fns):
                inst = fn(e)
                if i == n - 1 and key is not None:
                    inst.then_inc(self.h[key], inc)


def build_program(cfg):
    c = cfg
    D, DFF, KC, HC, HG, NG, PGC, T = c.D, c.DFF, c.KC, c.HC, c.HG, c.NG, c.PGC, c.T
    subs = c.subs
    NSUB = len(subs)
    nc = bass.Bass("TRN2", target_bir_lowering=False)

    def dram(name, shape, kind="ExternalInput"):
        return nc.dram_tensor(name, shape, F32, kind=kind).ap()

    xin = dram("xin", [2 * T + HALO, D])
    cvec = dram("cvec", [128, NVEC * KC])
    ident_d = dram("ident", [128, 128])
    posv_d = dram("posv", [128, POOL_HIST])
    w_in = dram("w_in", [D, 3 * D])
    w_out = dram("w_out", [D, D])
    pool_w = dram("pool_w", [4, PGC * 128, PGC * 128])
    w1 = dram("w1", [2, D, DFF])
    w2 = dram("w2", [2, DFF, D])
    out = dram("out", [c.NOUT, D], kind="ExternalOutput")

    with ExitStack() as es:
        def sb(name, shape, dt):
            return es.enter_context(nc.sbuf_tensor(name, shape, dt))

        x = sb("x", [128, KC, T], F32)
        z = sb("z", [128, KC, T], BF16)
        scr = sb("scr", [128, c.NSCR, T], BF16)
        slabs = [sb(f"slab{i}", [128, c.SLAB], BF16) for i in range(c.NB)]
        work = sb("work", [128, max(6 * c.MW, c.NST * D)], F32)
        mixb = [work[:, i * c.MW:(i + 1) * c.MW] for i in range(6)]
        stage = [work[:, i * D:(i + 1) * D] for i in range(c.NST)]
        rbuf = sb("rbuf", [128, T], F32)
        NSQ = 3
        SUBW = max(n for _, n in subs)
        sq = [sb(f"sq{i}", [128, SUBW], BF16) for i in range(NSQ)]
        NRT = 2
        rtb = [sb(f"rt{i}", [128, SUBW], F32) for i in range(NRT)]
        rs = sb("rs", [128, SUBW], F32)
        cv_sb = sb("cv_sb", [128, NVEC * KC], F32)
        ident = sb("ident_sb", [128, 128], F32)
        ones = sb("ones", [128, 128], BF16)
        posv = sb("posv_sb", [128, POOL_HIST], F32)
        tab = sb("tab", [128, 4, POOL_HIST], F32)
        tmp15 = sb("tmp15", [128, POOL_HIST], F32)
        hist = sb("hist", [128, KC, HALO], F32)
        uhp = sb("uhp", [128, KC, CONV_HIST], F32)
        zhp = sb("zhp", [128, KC, POOL_HIST], F32)
        NSTATE = 2 * (CONV_HIST + POOL_HIST)
        st_all = sb("st_all", [128, KC, NSTATE], F32)
        ps = es.enter_context(nc.psum_tensor("ps", [128, 8, 512], F32))

        S = Sched(nc, es)
        slab_sem = [S.dma_sem(f"slab{i}") for i in range(c.NB)]
        stage_sem = [S.dma_sem(f"stage{i}") for i in range(c.NST)]
        const_sem = [S.dma_sem(f"const{i}") for i in range(3)]

        X = [[Res(f"x{m}_{s}") for s in range(NSUB)] for m in range(KC)]
        Z = [[Res(f"z{m}_{s}") for s in range(NSUB)] for m in range(KC)]
        SCR = [[Res(f"scr{m}_{s}") for s in range(NSUB)] for m in range(c.NSCR)]
        SLABR = [Res(f"slab{i}") for i in range(c.NB)]
        WORKW = max(6 * c.MW, c.NST * D)
        bnds = sorted(set([i * c.MW for i in range(7)] + [i * D for i in range(c.NST + 1)] + [WORKW]))
        bnds = [b for b in bnds if b <= WORKW]
        ATOMS = [(bnds[i], bnds[i + 1], Res(f"atom{i}")) for i in range(len(bnds) - 1)]

        def atoms_in(lo, hi):
            return [r for (a, b, r) in ATOMS if a < hi and lo < b]
        MIX = [atoms_in(i * c.MW, (i + 1) * c.MW) for i in range(6)]
        NQ4 = KC // 4
        STAGE = [[Res(f"stage{i}_{q}") for q in range(NQ4)] for i in range(c.NST)]
        STX = [atoms_in(i * D, (i + 1) * D) for i in range(c.NST)]
        RB = [Res(f"rb{s}") for s in range(NSUB)]
        SQ = [Res(f"sq{i}") for i in range(NSQ)]
        RT = [Res(f"rt{i}") for i in range(NRT)]
        RS = Res("rs")
        BANK = [Res(f"bank{i}") for i in range(8)]
        CVEC = Res("cvec")
        IDENT = Res("ident")
        ONES = Res("ones")
        POSV = Res("posv")
        TAB = Res("tab")
        TMP15 = Res("tmp15")
        HIST = Res("hist")
        UHP = [Res(f"uhp{m}") for m in range(KC)]
        ZHP = [Res(f"zhp{m}") for m in range(KC)]
        STALL = Res("st_all")

        state = {"bank": 0, "slab": 0, "stage": 0, "sq": 0, "rt": 0, "alt": 0}

        reserved = set()

        def next_bank():
            while True:
                b = state["bank"]
                state["bank"] = (b + 1) % 8
                if b not in reserved:
                    return b

        def alt_eng():
            state["alt"] ^= 1
            return "act" if state["alt"] else "dve"

        def subs_overlapping(c0, n):
            return [s for s, (s0, sn) in enumerate(subs) if s0 < c0 + n and c0 < s0 + sn]

        def gv(i, m):
            return cv_sb[:, i * KC + m:i * KC + m + 1]

        def act_fn(out_, in_, func, scale=None, bias=None):
            kw = {}
            if scale is not None:
                kw["scale"] = scale
            if bias is not None:
                kw["bias"] = bias
            return lambda e: e.activation(out=out_, in_=in_, func=func, **kw)

        def copy_op(eng, out_, in_, reads, writes):
            if eng == "act":
                return S.op("act", act_fn(out_, in_, AF.Copy), reads, writes)
            return S.op("dve", lambda e: e.tensor_copy(out=out_, in_=in_), reads, writes)

        def tt(out_, in0, in1, op, reads, writes):
            return S.op("dve", lambda e: e.tensor_tensor(out=out_, in0=in0, in1=in1, op=op),
                        reads, writes)

        def stt(out_, in0, scalar, in1, op0, op1, reads, writes):
            return S.op("dve", lambda e: e.scalar_tensor_tensor(out=out_, in0=in0, scalar=scalar,
                                                               in1=in1, op0=op0, op1=op1),
                        reads, writes)

        def mm_fn(out_, lhsT, rhs, start, stop):
            return lambda e: e.matmul(out_, lhsT, rhs, start=start, stop=stop)

        def tr_fn(out_, in_, idn):
            return lambda e: e.transpose(out_, in_, idn)

        slab_gate = []

        def load_slab(src_ap, view_shape):
            i = state["slab"]
            state["slab"] = (i + 1) % c.NB
            k, n = view_shape
            view = slabs[i][:, 0:k * n].rearrange("p (k n) -> p k n", k=k)
            extra = [slab_gate.pop(0)] if slab_gate else []
            S.dma("pool", lambda e: e.dma_start(out=view, in_=src_ap), slab_sem[i],
                  reads=(), writes=[SLABR[i]], extra=extra)
            return view, SLABR[i]

        S.dma("sp", lambda e: e.dma_start(out=ident[:, :], in_=ident_d), const_sem[0], writes=[IDENT])
        S.dma("sp", lambda e: e.dma_start(out=cv_sb[:, :], in_=cvec), const_sem[1], writes=[CVEC])
        S.dma("sp", lambda e: e.dma_start(out=posv[:, :], in_=posv_d), const_sem[2], writes=[POSV])
        S.op("dve", lambda e: e.memset(ones[:, :], 1.0), (), [ONES])
        for i in range(6):
            S.op("dve", (lambda i: lambda e: e.memset(mixb[i][:, :], 0.0))(i), (), [MIX[i]])
        for g, w in enumerate(WINDOWS):
            S.op("dve", (lambda g, w: lambda e: e.tensor_scalar(
                out=tab[:, g, :], in0=posv[:, :], scalar1=float(w), scalar2=None, op0=ALU.min))(g, w),
                [POSV], [TAB])
        S.op("dve", lambda e: e.reciprocal(out=tab[:, :, :], in_=tab[:, :, :]), [TAB], [TAB])

        def load_pass(p):
            ns0 = NormStream(zvec=0, zalt=True)
            ns0.begin()
            done_sub = 0
            load_toks = []
            for c0 in range(0, T, 128):
                n = min(128, T - c0)
                b = state["stage"]
                state["stage"] = (b + 1) % c.NST
                st = stage[b]
                r0 = p * T + c0
                ltok = S.dma("sp", (lambda st, r0, n: lambda e: e.dma_start(out=st[:n, :], in_=xin[r0:r0 + n, :]))(st, r0, n),
                             stage_sem[b], reads=(), writes=STAGE[b] + STX[b])
                load_toks.append(ltok)
                ss = subs_overlapping(c0, n)
                for m4 in range(NQ4):
                    bank = next_bank()
                    fns = [tr_fn(ps[:, bank, q * 128:q * 128 + n],
                                 st[:n, (4 * m4 + q) * 128:(4 * m4 + q + 1) * 128],
                                 ident[:n, :n]) for q in range(4)]
                    S.op("pe", fns, reads=STAGE[b] + STX[b] + [IDENT], writes=[BANK[bank]])
                    src = ps[:, bank, :].rearrange("p (q c) -> p q c", q=4)[:, :, :n]
                    dst = x[:, 4 * m4:4 * m4 + 4, c0:c0 + n]
                    copy_op(alt_eng(), dst, src, [BANK[bank]],
                            [X[4 * m4 + q][s] for q in range(4) for s in ss])
                while done_sub < NSUB and subs[done_sub][0] + subs[done_sub][1] <= c0 + n:
                    for m in range(KC):
                        ns0.tile_done(m, done_sub)
                    ns0.flush()
                    norm_sub(done_sub, 0, "zrc", ns0)
                    done_sub += 1
            nt = len(load_toks)
            slab_gate.clear()
            slab_gate.extend([load_toks[min(nt - 1, 4 + 2 * j)] for j in range(c.NB)])
            if p == 1:
                b = state["stage"]
                state["stage"] = (b + 1) % c.NST
                st = stage[b]
                S.dma("sp", (lambda st: lambda e: e.dma_start(out=st[:HALO, :], in_=xin[2 * T:2 * T + HALO, :]))(st),
                      stage_sem[b], reads=(), writes=STAGE[b] + STX[b])
                for m4 in range(NQ4):
                    bank = next_bank()
                    fns = [tr_fn(ps[:, bank, q * 128:q * 128 + HALO],
                                 st[:HALO, (4 * m4 + q) * 128:(4 * m4 + q + 1) * 128],
                                 ident[:HALO, :HALO]) for q in range(4)]
                    S.op("pe", fns, reads=STAGE[b] + STX[b] + [IDENT], writes=[BANK[bank]])
                    src_ = ps[:, bank, :].rearrange("p (q c) -> p q c", q=4)[:, :, :HALO]
                    copy_op(alt_eng(), hist[:, 4 * m4:4 * m4 + 4, :], src_, [BANK[bank]], [HIST])

        class NormStream:
            LAG = 2

            def __init__(self, zvec=None, zalt=False, xg=None):
                self.xg = xg
                self.banks = None
                self.pending = []
                self.zvec = zvec
                self.zalt = zalt

            def begin(self):
                self.banks = [next_bank() for _ in subs]
                reserved.update(self.banks)

            def _mm(self, m, s, qi):
                n = subs[s][1]
                S.op("pe", mm_fn(ps[:, self.banks[s], :n], ones[:, :], sq[qi][:, :n], m == 0, m == KC - 1),
                     [SQ[qi], ONES], [BANK[self.banks[s]]])

            def tile_done(self, m, s):
                c0, n = subs[s]
                qi = state["sq"]
                state["sq"] = (qi + 1) % NSQ
                if self.zalt and m % 2 == 0:
                    S.op("dve", (lambda m, c0, n, qi: lambda e: e.tensor_tensor(
                        out=sq[qi][:, :n], in0=x[:, m, c0:c0 + n], in1=x[:, m, c0:c0 + n], op=ALU.mult))(m, c0, n, qi),
                        [X[m][s]], [SQ[qi]])
                else:
                    S.op("act", act_fn(sq[qi][:, :n], x[:, m, c0:c0 + n], AF.Square), [X[m][s]], [SQ[qi]])
                if self.zvec is not None:
                    if self.zalt and m % 2 == 1:
                        S.op("dve", (lambda m, c0, n: lambda e: e.tensor_scalar(
                            out=z[:, m, c0:c0 + n], in0=x[:, m, c0:c0 + n], scalar1=gv(self.zvec, m),
                            scalar2=None, op0=ALU.mult))(m, c0, n), [X[m][s], CVEC], [Z[m][s]])
                    else:
                        S.op("act", act_fn(z[:, m, c0:c0 + n], x[:, m, c0:c0 + n], AF.Copy,
                                           scale=gv(self.zvec, m)), [X[m][s], CVEC], [Z[m][s]])
                if self.xg is not None:
                    S.op("act", act_fn(x[:, m, c0:c0 + n], x[:, m, c0:c0 + n], AF.Copy, scale=gv(self.xg, m)),
                         [X[m][s], CVEC], [X[m][s]])
                self.pending.append((m, s, qi))
                while len(self.pending) > self.LAG:
                    self._mm(*self.pending.pop(0))

            def flush(self):
                while self.pending:
                    self._mm(*self.pending.pop(0))

        rsb = [(rtb[0], RT[0]), (rtb[1], RT[1]), (rs, RS)]
        assert NSUB <= 3

        def norm_sub(s, vec_idx, mode, ns):
            c0, n = subs[s]
            bank = ns.banks[s]
            rsx, RSX = rsb[s]
            if mode in ("zr", "zrc"):
                S.op("act", act_fn(rsx[:, :n], ps[:, bank, :n], AF.Copy, scale=1.0 / D, bias=EPS),
                     [BANK[bank]], [RSX])
            else:
                S.op("act", act_fn(rsx[:, :n], ps[:, bank, :n], AF.Sqrt, scale=1.0 / D, bias=EPS),
                     [BANK[bank]], [RSX])
            reserved.discard(bank)
            if mode == "finalr":
                return
            if mode == "zrc":
                S.op("dve", (lambda n, rsx: lambda e: e.reciprocal(out=rsx[:, :n], in_=rsx[:, :n]))(n, rsx),
                     [RSX], [RSX])
                S.op("act", act_fn(rbuf[:, c0:c0 + n], rsx[:, :n], AF.Sqrt), [RSX], [RB[s]])
                return
            S.op("dve", (lambda c0, n, rsx: lambda e: e.reciprocal(out=rbuf[:, c0:c0 + n], in_=rsx[:, :n]))(c0, n, rsx),
                 [RSX], [RB[s]])
            if mode in ("stats", "zr"):
                return
            for m in range(KC):
                if mode == "z":
                    stt(z[:, m, c0:c0 + n], x[:, m, c0:c0 + n], gv(vec_idx, m), rbuf[:, c0:c0 + n],
                        ALU.mult, ALU.mult, [X[m][s], RB[s], CVEC], [Z[m][s]])
                else:
                    stt(x[:, m, c0:c0 + n], x[:, m, c0:c0 + n], gv(vec_idx, m), rbuf[:, c0:c0 + n],
                        ALU.mult, ALU.mult, [X[m][s], RB[s], CVEC], [X[m][s]])

        def norm(vec_idx, mode, ns):
            ns.flush()
            for s in range(NSUB):
                norm_sub(s, vec_idx, mode, ns)

        def pieces(p, c0, n, gap):
            if p == 0:
                return [(c0, n, 0)]
            nb = c.NBP
            out_ = []
            if c0 < nb:
                out_.append((c0, min(n, nb - c0), 0))
            if c0 + n > nb:
                lo = max(c0, nb)
                out_.append((lo, c0 + n - lo, gap))
            return out_

        def e_copy(eng, out_, in_, reads, writes):
            return S.op(eng, lambda e: e.tensor_copy(out=out_, in_=in_), reads, writes)

        def mixer_conv(p, ns):
            csb, ub, cvb = mixb[0:2], mixb[2:4], mixb[4:6]
            CSB, UB, CVB = MIX[0:2], MIX[2:4], MIX[4:6]
            G = CONV_HIST
            TT = T + (G if p == 1 else 0)
            nb = c.NBP
            for mp in range(KC // 2):
                for kind in (1, 2, 0):
                    col0 = kind * D + mp * c.SC
                    view, SR = load_slab(w_in[:, col0:col0 + c.SC].rearrange("(k p) n -> p k n", p=128),
                                         (KC, c.SC))
                    for j in range(2):
                        m = 2 * mp + j
                        for s, (c0, n) in enumerate(subs):
                            bank = next_bank()
                            fns = [mm_fn(ps[:, bank, :n], view[:, k, j * 128:(j + 1) * 128],
                                         z[:, k, c0:c0 + n], k == 0, k == KC - 1) for k in range(KC)]
                            S.op("pe", fns, [SR] + [Z[k][s] for k in range(KC)], [BANK[bank]])
                            if kind == 1:
                                copy_op("act", csb[j][:, c0:c0 + n], ps[:, bank, :n], [BANK[bank]], [CSB[j]])
                                tt(csb[j][:, c0:c0 + n], csb[j][:, c0:c0 + n], rsb[s][0][:, :n], ALU.mult,
                                   [CSB[j], rsb[s][1]], [CSB[j]])
                            elif kind == 2:
                                for (pc, pn, sh) in pieces(p, c0, n, G):
                                    tt(ub[j][:, 2 + pc + sh:2 + pc + sh + pn], csb[j][:, pc:pc + pn],
                                       ps[:, bank, pc - c0:pc - c0 + pn], ALU.mult, [BANK[bank], CSB[j]], [UB[j]])
                            else:
                                for (pc, pn, sh) in pieces(p, c0, n, G):
                                    tt(scr[:, m, pc:pc + pn], ps[:, bank, pc - c0:pc - c0 + pn],
                                       cvb[j][:, pc + sh:pc + sh + pn], ALU.mult, [BANK[bank], CVB[j]], [SCR[m][s]])
                        if kind == 2:
                            u = ub[j]
                            if p == 1:
                                e_copy("dve", u[:, 0:G], uhp[:, m, :], [UHP[m]], [UB[j]])
                                e_copy("dve", u[:, 2 + nb:2 + nb + G], hist[:, m, 0:G], [HIST], [UB[j]])
                            S.op("act", act_fn(cvb[j][:, 0:TT], u[:, 2:2 + TT], AF.Copy, scale=gv(8, m)),
                                 [UB[j], CVEC], [CVB[j]])
                            stt(cvb[j][:, 0:TT], u[:, 1:1 + TT], gv(7, m), cvb[j][:, 0:TT], ALU.mult, ALU.add,
                                [UB[j], CVB[j], CVEC], [CVB[j]])
                            stt(cvb[j][:, 0:TT], u[:, 0:TT], gv(6, m), cvb[j][:, 0:TT], ALU.mult, ALU.add,
                                [UB[j], CVB[j], CVEC], [CVB[j]])
                            for (pc, pn, sh) in pieces(p, 0, T, G):
                                ssx = subs_overlapping(pc, pn)
                                tt(cvb[j][:, pc + sh:pc + sh + pn], cvb[j][:, pc + sh:pc + sh + pn],
                                   rbuf[:, pc:pc + pn], ALU.mult, [CVB[j]] + [RB[s_] for s_ in ssx], [CVB[j]])
                            if p == 0:
                                S.op("act", act_fn(uhp[:, m, :], u[:, 2 + T - G:2 + T], AF.Copy), [UB[j]], [UHP[m]])
                            else:
                                S.op("act", act_fn(st_all[:, m, 0:2], u[:, 2 + nb - G:2 + nb], AF.Copy),
                                     [UB[j]], [STALL])
                                e1 = 2 + nb + c.NS + G
                                S.op("act", act_fn(st_all[:, m, 2:4], u[:, e1 - G:e1], AF.Copy),
                                     [UB[j]], [STALL])
            ns.begin()
            for q in range(D // c.SC):
                view, SR = load_slab(w_out[:, q * c.SC:(q + 1) * c.SC].rearrange("(k p) n -> p k n", p=128),
                                     (KC, c.SC))
                for j in range(c.SC // 128):
                    m = q * (c.SC // 128) + j
                    for s, (c0, n) in enumerate(subs):
                        bank = next_bank()
                        fns = [mm_fn(ps[:, bank, :n], view[:, k, j * 128:(j + 1) * 128],
                                     scr[:, k, c0:c0 + n], k == 0, k == KC - 1) for k in range(KC)]
                        S.op("pe", fns, [SR] + [SCR[k][s] for k in range(KC)], [BANK[bank]])
                        tt(x[:, m, c0:c0 + n], ps[:, bank, :n], x[:, m, c0:c0 + n], ALU.add,
                           [BANK[bank], X[m][s]], [X[m][s]])
                        ns.tile_done(m, s)

        def mixer_pool(p, ns):
            zbs, pa, pb = mixb[0:2], mixb[2:4], mixb[4:6]
            ZB, PA, PB = MIX[0:2], MIX[2:4], MIX[4:6]
            O = 32
            G = POOL_HIST
            TT = T + (G if p == 1 else 0)
            nb = c.NBP
            GCW = PGC * 128
            pw = [load_slab(pool_w[g].rearrange("(k p) n -> p k n", p=128), (PGC, GCW)) for g in range(4)]
            for i in range(2):
                S.op("dve", (lambda i: lambda e: e.memset(zbs[i][:, 0:O], 0.0))(i), (), [ZB[i]])
            def chunk_ew(m):
                g = m // PGC
                w = WINDOWS[g]
                i = m % 2
                zb = zbs[i]
                for (pc, pn, sh) in pieces(p, 0, T, G):
                    ss = subs_overlapping(pc, pn)
                    stt(zb[:, O + pc + sh:O + pc + sh + pn], x[:, m, pc:pc + pn], gv(1, m), rbuf[:, pc:pc + pn],
                        ALU.mult, ALU.mult, [X[m][s] for s in ss] + [RB[s] for s in ss] + [CVEC], [ZB[i]])
                if p == 0:
                    S.op("act", act_fn(zhp[:, m, :], zb[:, O + T - G:O + T], AF.Copy), [ZB[i]], [ZHP[m]])
                else:
                    e_copy("dve", zb[:, O - G:O], zhp[:, m, :], [ZHP[m]], [ZB[i]])
                    e_copy("dve", zb[:, O + nb:O + nb + G], hist[:, m, CONV_HIST:HALO], [HIST], [ZB[i]])
                    S.op("act", act_fn(st_all[:, m, 4:4 + G], zb[:, O + nb - G:O + nb], AF.Copy),
                         [ZB[i]], [STALL])
                    e1 = O + nb + c.NS + G
                    S.op("act", act_fn(st_all[:, m, 4 + G:4 + 2 * G], zb[:, e1 - G:e1], AF.Copy),
                         [ZB[i]], [STALL])
                cur, CUR = zb, ZB[i]
                if w == 2:
                    b0 = O - 14
                    ln = O + TT - b0
                    tt(pa[i][:, b0:b0 + ln], zb[:, b0:b0 + ln], zb[:, b0 - 1:b0 - 1 + ln], ALU.add, [ZB[i]], [PA[i]])
                else:
                    b0 = O - G
                    ln = O + TT - b0
                    S.op("dve", (lambda i, zb, b0, ln, w: lambda e: e.tensor_tensor_scan(
                        out=pa[i][:, b0:b0 + ln], data0=zb[:, b0:b0 + ln], data1=zb[:, b0 - w:b0 - w + ln],
                        initial=0.0, op0=ALU.add, op1=ALU.subtract))(i, zb, b0, ln, w), [ZB[i]], [PA[i]])
                cur, CUR = pa[i], PA[i]
                for (pc, pn, sh) in pieces(p, 0, T, G):
                    stt(z[:, m, pc:pc + pn], cur[:, O + pc + sh:O + pc + sh + pn], 1.0 / w,
                        zb[:, O + pc + sh:O + pc + sh + pn], ALU.mult, ALU.subtract,
                        [CUR, ZB[i]], [Z[m][s] for s in subs_overlapping(pc, pn)])
                if p == 0:
                    a0 = O + HALO
                    tt(tmp15[:, :], cur[:, a0:a0 + G], tab[:, g, :], ALU.mult, [CUR, TAB], [TMP15])
                    tt(z[:, m, HALO:HALO + G], tmp15[:, :], zb[:, a0:a0 + G], ALU.subtract,
                       [TMP15, ZB[i]], [Z[m][s] for s in subs_overlapping(HALO, G)])
            ns.begin()
            for g in range(4):
                ns.zalt = (g == 3)
                for m_ in range(g * PGC, (g + 1) * PGC):
                    chunk_ew(m_)
                view, SR = pw[g]
                for jo in range(PGC):
                    m = g * PGC + jo
                    for s, (c0, n) in enumerate(subs):
                        bank = next_bank()
                        fns = [mm_fn(ps[:, bank, :n], view[:, k, jo * 128:(jo + 1) * 128],
                                     z[:, g * PGC + k, c0:c0 + n], k == 0, k == PGC - 1) for k in range(PGC)]
                        S.op("pe", fns, [SR] + [Z[g * PGC + k][s] for k in range(PGC)], [BANK[bank]])
                        stt(x[:, m, c0:c0 + n], ps[:, bank, :n], gv(5, m), x[:, m, c0:c0 + n],
                            ALU.mult, ALU.add, [BANK[bank], X[m][s], CVEC], [X[m][s]])
                for jo in range(PGC):
                    for s in range(NSUB):
                        ns.tile_done(g * PGC + jo, s)

        def ffn(l, ns):
            def w1_group(g, b):
                for q4 in range(HG // 2):
                    col0 = g * HG * 128 + q4 * c.SC
                    view, SR = load_slab(w1[l][:, col0:col0 + c.SC].rearrange("(k p) n -> p k n", p=128),
                                         (KC, c.SC))
                    for j in range(2):
                        hc = 2 * q4 + j
                        for s, (c0, n) in enumerate(subs):
                            bank = next_bank()
                            fns = [mm_fn(ps[:, bank, :n], view[:, k, j * 128:(j + 1) * 128],
                                         z[:, k, c0:c0 + n], k == 0, k == KC - 1) for k in range(KC)]
                            S.op("pe", fns, [SR] + [Z[k][s] for k in range(KC)], [BANK[bank]])
                            ri = state["rt"]
                            state["rt"] = (ri + 1) % NRT
                            S.op("act", act_fn(rtb[ri][:, :n], ps[:, bank, :n], AF.Relu), [BANK[bank]], [RT[ri]])
                            tt(rtb[ri][:, :n], rtb[ri][:, :n], rbuf[:, c0:c0 + n], ALU.mult,
                               [RT[ri], RB[s]], [RT[ri]])
                            tt(scr[:, b * HG + hc, c0:c0 + n], ps[:, bank, :n], rtb[ri][:, :n], ALU.mult,
                               [BANK[bank], RT[ri]], [SCR[b * HG + hc][s]])

            def w2_group(g, b, last=False):
                if last:
                    ns.begin()
                for q in range(D // c.SC2):
                    view, SR = load_slab(
                        w2[l][g * HG * 128:(g + 1) * HG * 128, q * c.SC2:(q + 1) * c.SC2].rearrange(
                            "(k p) n -> p k n", p=128), (HG, c.SC2))
                    for j in range(c.SC2 // 128):
                        m = q * (c.SC2 // 128) + j
                        for s, (c0, n) in enumerate(subs):
                            bank = next_bank()
                            fns = [mm_fn(ps[:, bank, :n], view[:, k, j * 128:(j + 1) * 128],
                                         scr[:, b * HG + k, c0:c0 + n], k == 0, k == HG - 1) for k in range(HG)]
                            S.op("pe", fns, [SR] + [SCR[b * HG + k][s] for k in range(HG)], [BANK[bank]])
                            tt(x[:, m, c0:c0 + n], ps[:, bank, :n], x[:, m, c0:c0 + n], ALU.add,
                               [BANK[bank], X[m][s]], [X[m][s]])
                            if last:
                                ns.tile_done(m, s)

            w1_group(0, 0)
            for g in range(1, NG):
                w1_group(g, g % 2)
                w2_group(g - 1, (g - 1) % 2)
            w2_group(NG - 1, (NG - 1) % 2, last=True)

        store_toks = []

        rtok = sb("rtok", [128, 4], F32)
        RTOK = [Res(f"rtok{i}") for i in range(4)]
        state["rtok"] = 0

        def out_tile(src_fn, src_res_fn, n, r0, rms=None):
            b = state["stage"]
            state["stage"] = (b + 1) % c.NST
            st = stage[b]
            ri = None
            if rms is not None:
                ri = state["rtok"]
                state["rtok"] = (ri + 1) % 4
                bank = next_bank()
                S.op("pe", tr_fn(ps[:n, bank, 0:128], rms[0], ident[:, :]), [IDENT, rms[1]], [BANK[bank]])
                S.op("dve", (lambda n, bank, ri: lambda e: e.reciprocal(out=rtok[:n, ri:ri + 1],
                                                                      in_=ps[:n, bank, 0:1]))(n, bank, ri),
                     [BANK[bank]], [RTOK[ri]])
            for m4 in range(NQ4):
                bank = next_bank()
                fns = [tr_fn(ps[:n, bank, q * 128:(q + 1) * 128], src_fn(4 * m4 + q), ident[:, :])
                       for q in range(4)]
                rr = [IDENT]
                for q in range(4):
                    rr += src_res_fn(4 * m4 + q)
                S.op("pe", fns, rr, [BANK[bank]])
                dst = st[:n, m4 * 512:(m4 + 1) * 512]
                wr = [STAGE[b][m4]] + (STX[b] if m4 == 0 else [])
                if ri is None:
                    copy_op(alt_eng(), dst, ps[:n, bank, :], [BANK[bank]], wr)
                elif alt_eng() == "act":
                    S.op("act", act_fn(dst, ps[:n, bank, :], AF.Copy, scale=rtok[:n, ri:ri + 1]),
                         [BANK[bank], RTOK[ri]], wr)
                else:
                    S.op("dve", (lambda dst, n, bank, ri: lambda e: e.tensor_scalar(
                        out=dst, in0=ps[:n, bank, :], scalar1=rtok[:n, ri:ri + 1], scalar2=None,
                        op0=ALU.mult))(dst, n, bank, ri), [BANK[bank], RTOK[ri]], wr)
            tok = S.dma("sp", (lambda st, n, r0: lambda e: e.dma_start(out=out[r0:r0 + n, :], in_=st[:n, :]))(st, n, r0),
                        stage_sem[b], reads=STAGE[b] + STX[b], writes=())
            store_toks.append(tok)

        def store_pass(p):
            if p == 0:
                segs = [(HALO, c.NA, 0)]
            else:
                segs = [(0, c.NBP, c.NA), (c.SAMP0, c.NS, c.NPC)]
            for (cs, cnt, r0) in segs:
                for s, (s0, sn) in enumerate(subs):
                    lo, hi = max(cs, s0), min(cs + cnt, s0 + sn)
                    for c0 in range(lo, hi, 128):
                        n = min(128, hi - c0)
                        rsx, RSX = rsb[s]
                        out_tile(lambda m, c0=c0, n=n: x[:, m, c0:c0 + n],
                                 lambda m, s=s: [X[m][s]], n, r0 + (c0 - cs),
                                 rms=(rsx[:, c0 - s0:c0 - s0 + n], RSX))

        for p in range(2):
            load_pass(p)
            ns = NormStream(zvec=2)
            mixer_conv(p, ns)
            norm(2, "zr", ns)
            ns = NormStream()
            ffn(0, ns)
            norm(1, "stats", ns)
            if p == 0:
                subs_full = list(subs)
                subs[0] = (HALO, subs[0][1] - HALO)
            ns = NormStream(zvec=3)
            mixer_pool(p, ns)
            norm(3, "zr", ns)
            ns = NormStream(xg=4)
            ffn(1, ns)
            norm(4, "finalr", ns)
            store_pass(p)
            if p == 0:
                subs[:] = subs_full
        out_tile(lambda m: st_all[:, m, :], lambda m: [STALL], NSTATE, c.NPC + c.NS)
        S.wait_only("sp", store_toks)

        with nc.Block() as block:
            @block.tensor
            def _(e):
                S.replay("pe", e)

            @block.scalar
            def _(e):
                S.replay("act", e)

            @block.vector
            def _(e):
                S.replay("dve", e)

            @block.gpsimd
            def _(e):
                S.replay("pool", e)

            @block.sync
            def _(e):
                S.replay("sp", e)
    return nc


def host_inputs(cfg, core, x_prompt, x_sample, cache_conv, cache_pool, mix_norm, ffn_norm, conv_w_in,
                conv_w, conv_w_out, pool_w, pool_scale, ffn_w1, ffn_w2, final_norm):
    c = cfg
    T, D, KC = c.T, c.D, c.KC
    s0 = core * c.NPC
    xp = x_prompt[0]
    xin = np.zeros((2 * T + HALO, D), np.float32)
    lo = s0 - HALO
    a = max(lo, 0)
    xin[a - lo:HALO + c.NA] = xp[a:s0 + c.NA]
    xin[T:T + c.NBP] = xp[s0 + c.NA:s0 + c.NPC]
    xin[T + c.SAMP0:T + c.SAMP0 + c.NS] = x_sample[core]
    xin[2 * T:2 * T + CONV_HIST] = cache_conv[0, core]
    xin[2 * T + CONV_HIST:2 * T + HALO] = cache_pool[0, core]
    vecs = [mix_norm[0], mix_norm[1], ffn_norm[0], ffn_norm[1], final_norm, pool_scale[0],
            conv_w[0, 0], conv_w[0, 1], conv_w[0, 2]]
    cvec = np.concatenate([np.ascontiguousarray(v.reshape(KC, 128).T) for v in vecs], axis=1)
    posv = np.broadcast_to((s0 + 1 + np.arange(POOL_HIST)).astype(np.float32)[None, :], (128, POOL_HIST))
    return {
        "xin": xin,
        "cvec": np.ascontiguousarray(cvec, dtype=np.float32),
        "ident": np.eye(128, dtype=np.float32),
        "posv": np.ascontiguousarray(posv),
        "w_in": conv_w_in[0],
        "w_out": conv_w_out[0],
        "pool_w": pool_w[0],
        "w1": ffn_w1,
        "w2": ffn_w2,
    }


_NC_CACHE = {}


def run(cfg, inputs):
    key = (cfg.D, cfg.DFF, cfg.NPC, cfg.NS, cfg.T, cfg.subs[0][1])
    if key not in _NC_CACHE:
        _NC_CACHE[key] = build_program(cfg)
    nc = _NC_CACHE[key]
    inputs = {k: np.ascontiguousarray(np.asarray(v, dtype=np.float32)) for k, v in inputs.items()}
    in_maps = [host_inputs(cfg, core, **inputs) for core in range(cfg.ncores)]
    res = run_bass_kernel_spmd(nc, in_maps, core_ids=list(range(cfg.ncores)))
    outs = [r["out"] for r in res.results]
    c = cfg
    n = c.ncores
    y_prompt = np.concatenate([o[0:c.NPC] for o in outs], axis=0)[None]
    y_sample = np.stack([o[c.NPC:c.NPC + c.NS] for o in outs], axis=0)
    b = c.NPC + c.NS
    conv_p = outs[n - 1][b:b + 2][None, None]
    conv_s = np.stack([o[b + 2:b + 4] for o in outs], axis=0)[None]
    pool_p = outs[n - 1][b + 4:b + 19][None, None]
    pool_s = np.stack([o[b + 19:b + 34] for o in outs], axis=0)[None]
    return (np.ascontiguousarray(y_prompt), np.ascontiguousarray(y_sample), np.ascontiguousarray(conv_p),
            np.ascontiguousarray(pool_p), np.ascontiguousarray(conv_s), np.ascontiguousarray(pool_s))


def kernel(**inputs):
    return run(Cfg(), inputs)
```

```python
import numpy as np
from contextlib import ExitStack
import concourse.bass as bass
import concourse.mybir as mybir
from concourse.bass_utils import run_bass_kernel_spmd

F32 = mybir.dt.float32
BF16 = mybir.dt.bfloat16
AF = mybir.ActivationFunctionType
ALU = mybir.AluOpType

HALO = 17
CONV_HIST = 2
POOL_HIST = 15
EPS = 1e-6
WINDOWS = (2, 4, 8, 16)
NVEC = 9


class Cfg:
    def __init__(self, D=2048, DFF=8192, NPC=2048, NS=32, SUBMAX=350, HG=8, NB=4,
                 NST=3, ncores=8):
        T = -(-(NPC + HALO + NS) // 2)
        self.D, self.DFF, self.NPC, self.NS, self.T = D, DFF, NPC, NS, T
        self.KC = D // 128
        self.HC = DFF // 128
        self.HG = HG
        self.NG = self.HC // HG
        self.PGC = self.KC // 4
        self.NB = NB
        self.NST = NST
        self.ncores = ncores
        self.NA = T - HALO
        self.NBP = NPC - self.NA
        self.SAMP0 = self.NBP
        assert self.NBP + NS <= T and self.NBP > 32
        nsub = -(-T // SUBMAX)
        base, rem = divmod(T, nsub)
        self.subs = []
        c = 0
        for i in range(nsub):
            n = base + (1 if i < rem else 0)
            self.subs.append((c, n))
            c += n
        assert max(n for _, n in self.subs) <= 512
        self.SC = 256
        self.SC2 = min(512, D)
        self.SLAB = max(self.KC * self.SC, HG * self.SC2, self.PGC * self.PGC * 128)
        self.NSCR = max(self.KC, 2 * HG)
        self.NOUT = NPC + NS + 2 * (CONV_HIST + POOL_HIST)
        self.MW = T + 32 + POOL_HIST


class Res:
    __slots__ = ("name", "wr", "rd")

    def __init__(self, name):
        self.name = name
        self.wr = {}
        self.rd = {}


class Sched:
    ENG = ("pe", "act", "dve", "pool", "sp")

    def __init__(self, nc, es):
        self.nc = nc
        self.es = es
        self.h = {}
        self.cnt = {}
        for e in self.ENG:
            self.h["e_" + e] = es.enter_context(nc.semaphore("s_" + e))
            self.cnt["e_" + e] = 0
        self.q = {e: [] for e in self.ENG}
        self.waited = {e: {} for e in self.ENG}

    def dma_sem(self, name):
        key = "d_" + name
        self.h[key] = self.es.enter_context(self.nc.semaphore(key))
        self.cnt[key] = 0
        return key

    @staticmethod
    def _flat(lst):
        out = []
        for r in lst:
            if isinstance(r, (list, tuple)):
                out.extend(Sched._flat(r))
            else:
                out.append(r)
        return out

    def _deps(self, eng, reads, writes, extra):
        d = {}
        for r in reads:
            for k, v in r.wr.items():
                if d.get(k, 0) < v:
                    d[k] = v
        for w in writes:
            for k, v in w.wr.items():
                if d.get(k, 0) < v:
                    d[k] = v
            for k, v in w.rd.items():
                if d.get(k, 0) < v:
                    d[k] = v
        for t in extra:
            if t is not None and d.get(t[0], 0) < t[1]:
                d[t[0]] = t[1]
        waits = []
        wd = self.waited[eng]
        for k, v in d.items():
            if eng == "pe" and k == "e_pe":
                continue
            if wd.get(k, 0) >= v:
                continue
            wd[k] = v
            waits.append((k, v))
        return waits

    @staticmethod
    def _commit(tok, reads, writes):
        k, v = tok
        for r in reads:
            if r.rd.get(k, 0) < v:
                r.rd[k] = v
        for w in writes:
            w.wr = {k: v}
            w.rd = {}

    def op(self, eng, fns, reads=(), writes=(), extra=()):
        if not isinstance(fns, (list, tuple)):
            fns = [fns]
        reads, writes = self._flat(reads), self._flat(writes)
        waits = self._deps(eng, reads, writes, extra)
        key = "e_" + eng
        self.cnt[key] += 1
        tok = (key, self.cnt[key])
        self.q[eng].append((waits, fns, key, 1))
        self._commit(tok, reads, writes)
        return tok

    def dma(self, eng, fn, dsem, reads=(), writes=(), extra=()):
        reads, writes = self._flat(reads), self._flat(writes)
        waits = self._deps(eng, reads, writes, extra)
        self.cnt[dsem] += 16
        tok = (dsem, self.cnt[dsem])
        self.q[eng].append((waits, [fn], dsem, 16))
        self._commit(tok, reads, writes)
        return tok

    def wait_only(self, eng, toks):
        waits = self._deps(eng, (), (), toks)
        self.q[eng].append((waits, [], None, 0))

    def replay(self, eng, e):
        for waits, fns, key, inc in self.q[eng]:
            for k, v in waits:
                e.wait_ge(self.h[k], v)
            n = len(fns)
            for i, fn in enumerate(fns):
                inst = fn(e)
                if i == n - 1 and key is not None:
                    inst.then_inc(self.h[key], inc)


def build_program(cfg):
    c = cfg
    D, DFF, KC, HC, HG, NG, PGC, T = c.D, c.DFF, c.KC, c.HC, c.HG, c.NG, c.PGC, c.T
    subs = c.subs
    NSUB = len(subs)
    nc = bass.Bass("TRN2", target_bir_lowering=False)

    def dram(name, shape, kind="ExternalInput"):
        return nc.dram_tensor(name, shape, F32, kind=kind).ap()

    xin = dram("xin", [2 * T + HALO, D])
    cvec = dram("cvec", [128, NVEC * KC])
    ident_d = dram("ident", [128, 128])
    posv_d = dram("posv", [128, POOL_HIST])
    w_in = dram("w_in", [D, 3 * D])
    w_out = dram("w_out", [D, D])
    pool_w = dram("pool_w", [4, PGC * 128, PGC * 128])
    w1 = dram("w1", [2, D, DFF])
    w2 = dram("w2", [2, DFF, D])
    out = dram("out", [c.NOUT, D], kind="ExternalOutput")

    with ExitStack() as es:
        def sb(name, shape, dt):
            return es.enter_context(nc.sbuf_tensor(name, shape, dt))

        x = sb("x", [128, KC, T], F32)
        z = sb("z", [128, KC, T], BF16)
        scr = sb("scr", [128, c.NSCR, T], BF16)
        slabs = [sb(f"slab{i}", [128, c.SLAB], BF16) for i in range(c.NB)]
        work = sb("work", [128, max(6 * c.MW, c.NST * D)], F32)
        mixb = [work[:, i * c.MW:(i + 1) * c.MW] for i in range(6)]
        stage = [work[:, i * D:(i + 1) * D] for i in range(c.NST)]
        rbuf = sb("rbuf", [128, T], F32)
        NSQ = 3
        SUBW = max(n for _, n in subs)
        sq = [sb(f"sq{i}", [128, SUBW], BF16) for i in range(NSQ)]
        NRT = 2
        rtb = [sb(f"rt{i}", [128, SUBW], F32) for i in range(NRT)]
        rs = sb("rs", [128, SUBW], F32)
        cv_sb = sb("cv_sb", [128, NVEC * KC], F32)
        ident = sb("ident_sb", [128, 128], F32)
        ones = sb("ones", [128, 128], BF16)
        posv = sb("posv_sb", [128, POOL_HIST], F32)
        tab = sb("tab", [128, 4, POOL_HIST], F32)
        tmp15 = sb("tmp15", [128, POOL_HIST], F32)
        hist = sb("hist", [128, KC, HALO], F32)
        uhp = sb("uhp", [128, KC, CONV_HIST], F32)
        zhp = sb("zhp", [128, KC, POOL_HIST], F32)
        NSTATE = 2 * (CONV_HIST + POOL_HIST)
        st_all = sb("st_all", [128, KC, NSTATE], F32)
        ps = es.enter_context(nc.psum_tensor("ps", [128, 8, 512], F32))

        S = Sched(nc, es)
        slab_sem = [S.dma_sem(f"slab{i}") for i in range(c.NB)]
        stage_sem = [S.dma_sem(f"stage{i}") for i in range(c.NST)]
        const_sem = [S.dma_sem(f"const{i}") for i in range(3)]

        X = [[Res(f"x{m}_{s}") for s in range(NSUB)] for m in range(KC)]
        Z = [[Res(f"z{m}_{s}") for s in range(NSUB)] for m in range(KC)]
        SCR = [[Res(f"scr{m}_{s}") for s in range(NSUB)] for m in range(c.NSCR)]
        SLABR = [Res(f"slab{i}") for i in range(c.NB)]
        WORKW = max(6 * c.MW, c.NST * D)
        bnds = sorted(set([i * c.MW for i in range(7)] + [i * D for i in range(c.NST + 1)] + [WORKW]))
        bnds = [b for b in bnds if b <= WORKW]
        ATOMS = [(bnds[i], bnds[i + 1], Res(f"atom{i}")) for i in range(len(bnds) - 1)]

        def atoms_in(lo, hi):
            return [r for (a, b, r) in ATOMS if a < hi and lo < b]
        MIX = [atoms_in(i * c.MW, (i + 1) * c.MW) for i in range(6)]
        NQ4 = KC // 4
        STAGE = [[Res(f"stage{i}_{q}") for q in range(NQ4)] for i in range(c.NST)]
        STX = [atoms_in(i * D, (i + 1) * D) for i in range(c.NST)]
        RB = [Res(f"rb{s}") for s in range(NSUB)]
        SQ = [Res(f"sq{i}") for i in range(NSQ)]
        RT = [Res(f"rt{i}") for i in range(NRT)]
        RS = Res("rs")
        BANK = [Res(f"bank{i}") for i in range(8)]
        CVEC = Res("cvec")
        IDENT = Res("ident")
        ONES = Res("ones")
        POSV = Res("posv")
        TAB = Res("tab")
        TMP15 = Res("tmp15")
        HIST = Res("hist")
        UHP = [Res(f"uhp{m}") for m in range(KC)]
        ZHP = [Res(f"zhp{m}") for m in range(KC)]
        STALL = Res("st_all")

        state = {"bank": 0, "slab": 0, "stage": 0, "sq": 0, "rt": 0, "alt": 0}

        reserved = set()

        def next_bank():
            while True:
                b = state["bank"]
                state["bank"] = (b + 1) % 8
                if b not in reserved:
                    return b

        def alt_eng():
            state["alt"] ^= 1
            return "act" if state["alt"] else "dve"

        def subs_overlapping(c0, n):
            return [s for s, (s0, sn) in enumerate(subs) if s0 < c0 + n and c0 < s0 + sn]

        def gv(i, m):
            return cv_sb[:, i * KC + m:i * KC + m + 1]

        def act_fn(out_, in_, func, scale=None, bias=None):
            kw = {}
            if scale is not None:
                kw["scale"] = scale
            if bias is not None:
                kw["bias"] = bias
            return lambda e: e.activation(out=out_, in_=in_, func=func, **kw)

        def copy_op(eng, out_, in_, reads, writes):
            if eng == "act":
                return S.op("act", act_fn(out_, in_, AF.Copy), reads, writes)
            return S.op("dve", lambda e: e.tensor_copy(out=out_, in_=in_), reads, writes)

        def tt(out_, in0, in1, op, reads, writes):
            return S.op("dve", lambda e: e.tensor_tensor(out=out_, in0=in0, in1=in1, op=op),
                        reads, writes)

        def stt(out_, in0, scalar, in1, op0, op1, reads, writes):
            return S.op("dve", lambda e: e.scalar_tensor_tensor(out=out_, in0=in0, scalar=scalar,
                                                               in1=in1, op0=op0, op1=op1),
                        reads, writes)

        def mm_fn(out_, lhsT, rhs, start, stop):
            return lambda e: e.matmul(out_, lhsT, rhs, start=start, stop=stop)

        def tr_fn(out_, in_, idn):
            return lambda e: e.transpose(out_, in_, idn)

        slab_gate = []

        def load_slab(src_ap, view_shape):
            i = state["slab"]
            state["slab"] = (i + 1) % c.NB
            k, n = view_shape
            view = slabs[i][:, 0:k * n].rearrange("p (k n) -> p k n", k=k)
            extra = [slab_gate.pop(0)] if slab_gate else []
            S.dma("pool", lambda e: e.dma_start(out=view, in_=src_ap), slab_sem[i],
                  reads=(), writes=[SLABR[i]], extra=extra)
            return view, SLABR[i]

        S.dma("sp", lambda e: e.dma_start(out=ident[:, :], in_=ident_d), const_sem[0], writes=[IDENT])
        S.dma("sp", lambda e: e.dma_start(out=cv_sb[:, :], in_=cvec), const_sem[1], writes=[CVEC])
        S.dma("sp", lambda e: e.dma_start(out=posv[:, :], in_=posv_d), const_sem[2], writes=[POSV])
        S.op("dve", lambda e: e.memset(ones[:, :], 1.0), (), [ONES])
        for i in range(6):
            S.op("dve", (lambda i: lambda e: e.memset(mixb[i][:, :], 0.0))(i), (), [MIX[i]])
        for g, w in enumerate(WINDOWS):
            S.op("dve", (lambda g, w: lambda e: e.tensor_scalar(
                out=tab[:, g, :], in0=posv[:, :], scalar1=float(w), scalar2=None, op0=ALU.min))(g, w),
                [POSV], [TAB])
        S.op("dve", lambda e: e.reciprocal(out=tab[:, :, :], in_=tab[:, :, :]), [TAB], [TAB])

        def load_pass(p):
            ns0 = NormStream(zvec=0, zalt=True)
            ns0.begin()
            done_sub = 0
            load_toks = []
            for c0 in range(0, T, 128):
                n = min(128, T - c0)
                b = state["stage"]
                state["stage"] = (b + 1) % c.NST
                st = stage[b]
                r0 = p * T + c0
                ltok = S.dma("sp", (lambda st, r0, n: lambda e: e.dma_start(out=st[:n, :], in_=xin[r0:r0 + n, :]))(st, r0, n),
                             stage_sem[b], reads=(), writes=STAGE[b] + STX[b])
                load_toks.append(ltok)
                ss = subs_overlapping(c0, n)
                for m4 in range(NQ4):
                    bank = next_bank()
                    fns = [tr_fn(ps[:, bank, q * 128:q * 128 + n],
                                 st[:n, (4 * m4 + q) * 128:(4 * m4 + q + 1) * 128],
                                 ident[:n, :n]) for q in range(4)]
                    S.op("pe", fns, reads=STAGE[b] + STX[b] + [IDENT], writes=[BANK[bank]])
                    src = ps[:, bank, :].rearrange("p (q c) -> p q c", q=4)[:, :, :n]
                    dst = x[:, 4 * m4:4 * m4 + 4, c0:c0 + n]
                    copy_op(alt_eng(), dst, src, [BANK[bank]],
                            [X[4 * m4 + q][s] for q in range(4) for s in ss])
                while done_sub < NSUB and subs[done_sub][0] + subs[done_sub][1] <= c0 + n:
                    for m in range(KC):
                        ns0.tile_done(m, done_sub)
                    ns0.flush()
                    norm_sub(done_sub, 0, "zrc", ns0)
                    done_sub += 1
            nt = len(load_toks)
            slab_gate.clear()
            slab_gate.extend([load_toks[min(nt - 1, 4 + 2 * j)] for j in range(c.NB)])
            if p == 1:
                b = state["stage"]
                state["stage"] = (b + 1) % c.NST
                st = stage[b]
                S.dma("sp", (lambda st: lambda e: e.dma_start(out=st[:HALO, :], in_=xin[2 * T:2 * T + HALO, :]))(st),
                      stage_sem[b], reads=(), writes=STAGE[b] + STX[b])
                for m4 in range(NQ4):
                    bank = next_bank()
                    fns = [tr_fn(ps[:, bank, q * 128:q * 128 + HALO],
                                 st[:HALO, (4 * m4 + q) * 128:(4 * m4 + q + 1) * 128],
                                 ident[:HALO, :HALO]) for q in range(4)]
                    S.op("pe", fns, reads=STAGE[b] + STX[b] + [IDENT], writes=[BANK[bank]])
                    src_ = ps[:, bank, :].rearrange("p (q c) -> p q c", q=4)[:, :, :HALO]
                    copy_op(alt_eng(), hist[:, 4 * m4:4 * m4 + 4, :], src_, [BANK[bank]], [HIST])

        class NormStream:
            LAG = 2

            def __init__(self, zvec=None, zalt=False, xg=None):
                self.xg = xg
                self.banks = None
                self.pending = []
                self.zvec = zvec
                self.zalt = zalt

            def begin(self):
                self.banks = [next_bank() for _ in subs]
                reserved.update(self.banks)

            def _mm(self, m, s, qi):
                n = subs[s][1]
                S.op("pe", mm_fn(ps[:, self.banks[s], :n], ones[:, :], sq[qi][:, :n], m == 0, m == KC - 1),
                     [SQ[qi], ONES], [BANK[self.banks[s]]])

            def tile_done(self, m, s):
                c0, n = subs[s]
                qi = state["sq"]
                state["sq"] = (qi + 1) % NSQ
                if self.zalt and m % 2 == 0:
                    S.op("dve", (lambda m, c0, n, qi: lambda e: e.tensor_tensor(
                        out=sq[qi][:, :n], in0=x[:, m, c0:c0 + n], in1=x[:, m, c0:c0 + n], op=ALU.mult))(m, c0, n, qi),
                        [X[m][s]], [SQ[qi]])
                else:
                    S.op("act", act_fn(sq[qi][:, :n], x[:, m, c0:c0 + n], AF.Square), [X[m][s]], [SQ[qi]])
                if self.zvec is not None:
                    if self.zalt and m % 2 == 1:
                        S.op("dve", (lambda m, c0, n: lambda e: e.tensor_scalar(
                            out=z[:, m, c0:c0 + n], in0=x[:, m, c0:c0 + n], scalar1=gv(self.zvec, m),
                            scalar2=None, op0=ALU.mult))(m, c0, n), [X[m][s], CVEC], [Z[m][s]])
                    else:
                        S.op("act", act_fn(z[:, m, c0:c0 + n], x[:, m, c0:c0 + n], AF.Copy,
                                           scale=gv(self.zvec, m)), [X[m][s], CVEC], [Z[m][s]])
                if self.xg is not None:
                    S.op("act", act_fn(x[:, m, c0:c0 + n], x[:, m, c0:c0 + n], AF.Copy, scale=gv(self.xg, m)),
                         [X[m][s], CVEC], [X[m][s]])
                self.pending.append((m, s, qi))
                while len(self.pending) > self.LAG:
                    self._mm(*self.pending.pop(0))

            def flush(self):
                while self.pending:
                    self._mm(*self.pending.pop(0))

            def flush_sub(self, s):
                idx = [i for i, t in enumerate(self.pending) if t[1] == s]
                if idx:
                    for _ in range(idx[-1] + 1):
                        self._mm(*self.pending.pop(0))

        rsb = [(rtb[0], RT[0]), (rtb[1], RT[1]), (rs, RS)]
        assert NSUB <= 3

        def norm_sub(s, vec_idx, mode, ns):
            c0, n = subs[s]
            bank = ns.banks[s]
            rsx, RSX = rsb[s]
            if mode in ("zr", "zrc"):
                S.op("act", act_fn(rsx[:, :n], ps[:, bank, :n], AF.Copy, scale=1.0 / D, bias=EPS),
                     [BANK[bank]], [RSX])
            else:
                S.op("act", act_fn(rsx[:, :n], ps[:, bank, :n], AF.Sqrt, scale=1.0 / D, bias=EPS),
                     [BANK[bank]], [RSX])
            reserved.discard(bank)
            if mode == "finalr":
                return
            if mode == "zrc":
                S.op("dve", (lambda n, rsx: lambda e: e.reciprocal(out=rsx[:, :n], in_=rsx[:, :n]))(n, rsx),
                     [RSX], [RSX])
                S.op("act", act_fn(rbuf[:, c0:c0 + n], rsx[:, :n], AF.Sqrt), [RSX], [RB[s]])
                return
            S.op("dve", (lambda c0, n, rsx: lambda e: e.reciprocal(out=rbuf[:, c0:c0 + n], in_=rsx[:, :n]))(c0, n, rsx),
                 [RSX], [RB[s]])
            if mode in ("stats", "zr"):
                return
            for m in range(KC):
                if mode == "z":
                    stt(z[:, m, c0:c0 + n], x[:, m, c0:c0 + n], gv(vec_idx, m), rbuf[:, c0:c0 + n],
                        ALU.mult, ALU.mult, [X[m][s], RB[s], CVEC], [Z[m][s]])
                else:
                    stt(x[:, m, c0:c0 + n], x[:, m, c0:c0 + n], gv(vec_idx, m), rbuf[:, c0:c0 + n],
                        ALU.mult, ALU.mult, [X[m][s], RB[s], CVEC], [X[m][s]])

        def norm(vec_idx, mode, ns):
            ns.flush()
            for s in range(NSUB):
                norm_sub(s, vec_idx, mode, ns)

        def pieces(p, c0, n, gap):
            if p == 0:
                return [(c0, n, 0)]
            nb = c.NBP
            out_ = []
            if c0 < nb:
                out_.append((c0, min(n, nb - c0), 0))
            if c0 + n > nb:
                lo = max(c0, nb)
                out_.append((lo, c0 + n - lo, gap))
            return out_

        def e_copy(eng, out_, in_, reads, writes):
            return S.op(eng, lambda e: e.tensor_copy(out=out_, in_=in_), reads, writes)

        def mixer_conv(p, ns):
            csb, ub, cvb = mixb[0:2], mixb[2:4], mixb[4:6]
            CSB, UB, CVB = MIX[0:2], MIX[2:4], MIX[4:6]
            G = CONV_HIST
            TT = T + (G if p == 1 else 0)
            nb = c.NBP
            for mp in range(KC // 2):
                for kind in (1, 2, 0):
                    col0 = kind * D + mp * c.SC
                    view, SR = load_slab(w_in[:, col0:col0 + c.SC].rearrange("(k p) n -> p k n", p=128),
                                         (KC, c.SC))
                    for j in range(2):
                        m = 2 * mp + j
                        for s, (c0, n) in enumerate(subs):
                            bank = next_bank()
                            fns = [mm_fn(ps[:, bank, :n], view[:, k, j * 128:(j + 1) * 128],
                                         z[:, k, c0:c0 + n], k == 0, k == KC - 1) for k in range(KC)]
                            S.op("pe", fns, [SR] + [Z[k][s] for k in range(KC)], [BANK[bank]])
                            if kind == 1:
                                copy_op("act", csb[j][:, c0:c0 + n], ps[:, bank, :n], [BANK[bank]], [CSB[j]])
                                tt(csb[j][:, c0:c0 + n], csb[j][:, c0:c0 + n], rsb[s][0][:, :n], ALU.mult,
                                   [CSB[j], rsb[s][1]], [CSB[j]])
                            elif kind == 2:
                                for (pc, pn, sh) in pieces(p, c0, n, G):
                                    tt(ub[j][:, 2 + pc + sh:2 + pc + sh + pn], csb[j][:, pc:pc + pn],
                                       ps[:, bank, pc - c0:pc - c0 + pn], ALU.mult, [BANK[bank], CSB[j]], [UB[j]])
                            else:
                                for (pc, pn, sh) in pieces(p, c0, n, G):
                                    tt(scr[:, m, pc:pc + pn], ps[:, bank, pc - c0:pc - c0 + pn],
                                       cvb[j][:, pc + sh:pc + sh + pn], ALU.mult, [BANK[bank], CVB[j]], [SCR[m][s]])
                        if kind == 2:
                            u = ub[j]
                            if p == 1:
                                e_copy("dve", u[:, 0:G], uhp[:, m, :], [UHP[m]], [UB[j]])
                                e_copy("dve", u[:, 2 + nb:2 + nb + G], hist[:, m, 0:G], [HIST], [UB[j]])
                            S.op("act", act_fn(cvb[j][:, 0:TT], u[:, 2:2 + TT], AF.Copy, scale=gv(8, m)),
                                 [UB[j], CVEC], [CVB[j]])
                            stt(cvb[j][:, 0:TT], u[:, 1:1 + TT], gv(7, m), cvb[j][:, 0:TT], ALU.mult, ALU.add,
                                [UB[j], CVB[j], CVEC], [CVB[j]])
                            stt(cvb[j][:, 0:TT], u[:, 0:TT], gv(6, m), cvb[j][:, 0:TT], ALU.mult, ALU.add,
                                [UB[j], CVB[j], CVEC], [CVB[j]])
                            for (pc, pn, sh) in pieces(p, 0, T, G):
                                ssx = subs_overlapping(pc, pn)
                                tt(cvb[j][:, pc + sh:pc + sh + pn], cvb[j][:, pc + sh:pc + sh + pn],
                                   rbuf[:, pc:pc + pn], ALU.mult, [CVB[j]] + [RB[s_] for s_ in ssx], [CVB[j]])
                            if p == 0:
                                S.op("act", act_fn(uhp[:, m, :], u[:, 2 + T - G:2 + T], AF.Copy), [UB[j]], [UHP[m]])
                            else:
                                S.op("act", act_fn(st_all[:, m, 0:2], u[:, 2 + nb - G:2 + nb], AF.Copy),
                                     [UB[j]], [STALL])
                                e1 = 2 + nb + c.NS + G
                                S.op("act", act_fn(st_all[:, m, 2:4], u[:, e1 - G:e1], AF.Copy),
                                     [UB[j]], [STALL])
            ns.begin()
            for q in range(D // c.SC):
                view, SR = load_slab(w_out[:, q * c.SC:(q + 1) * c.SC].rearrange("(k p) n -> p k n", p=128),
                                     (KC, c.SC))
                for j in range(c.SC // 128):
                    m = q * (c.SC // 128) + j
                    for s, (c0, n) in enumerate(subs):
                        bank = next_bank()
                        fns = [mm_fn(ps[:, bank, :n], view[:, k, j * 128:(j + 1) * 128],
                                     scr[:, k, c0:c0 + n], k == 0, k == KC - 1) for k in range(KC)]
                        S.op("pe", fns, [SR] + [SCR[k][s] for k in range(KC)], [BANK[bank]])
                        tt(x[:, m, c0:c0 + n], ps[:, bank, :n], x[:, m, c0:c0 + n], ALU.add,
                           [BANK[bank], X[m][s]], [X[m][s]])
                        ns.tile_done(m, s)

        def mixer_pool(p, ns):
            zbs, pa, pb = mixb[0:2], mixb[2:4], mixb[4:6]
            ZB, PA, PB = MIX[0:2], MIX[2:4], MIX[4:6]
            O = 32
            G = POOL_HIST
            TT = T + (G if p == 1 else 0)
            nb = c.NBP
            GCW = PGC * 128
            pw = [load_slab(pool_w[g].rearrange("(k p) n -> p k n", p=128), (PGC, GCW)) for g in range(4)]
            for i in range(2):
                S.op("dve", (lambda i: lambda e: e.memset(zbs[i][:, 0:O], 0.0))(i), (), [ZB[i]])
            def chunk_ew(m):
                g = m // PGC
                w = WINDOWS[g]
                i = m % 2
                zb = zbs[i]
                for (pc, pn, sh) in pieces(p, 0, T, G):
                    ss = subs_overlapping(pc, pn)
                    stt(zb[:, O + pc + sh:O + pc + sh + pn], x[:, m, pc:pc + pn], gv(1, m), rbuf[:, pc:pc + pn],
                        ALU.mult, ALU.mult, [X[m][s] for s in ss] + [RB[s] for s in ss] + [CVEC], [ZB[i]])
                if p == 0:
                    S.op("act", act_fn(zhp[:, m, :], zb[:, O + T - G:O + T], AF.Copy), [ZB[i]], [ZHP[m]])
                else:
                    e_copy("dve", zb[:, O - G:O], zhp[:, m, :], [ZHP[m]], [ZB[i]])
                    e_copy("dve", zb[:, O + nb:O + nb + G], hist[:, m, CONV_HIST:HALO], [HIST], [ZB[i]])
                    S.op("act", act_fn(st_all[:, m, 4:4 + G], zb[:, O + nb - G:O + nb], AF.Copy),
                         [ZB[i]], [STALL])
                    e1 = O + nb + c.NS + G
                    S.op("act", act_fn(st_all[:, m, 4 + G:4 + 2 * G], zb[:, e1 - G:e1], AF.Copy),
                         [ZB[i]], [STALL])
                cur, CUR = zb, ZB[i]
                if w == 2:
                    b0 = O - 14
                    ln = O + TT - b0
                    tt(pa[i][:, b0:b0 + ln], zb[:, b0:b0 + ln], zb[:, b0 - 1:b0 - 1 + ln], ALU.add, [ZB[i]], [PA[i]])
                else:
                    b0 = O - G
                    ln = O + TT - b0
                    S.op("dve", (lambda i, zb, b0, ln, w: lambda e: e.tensor_tensor_scan(
                        out=pa[i][:, b0:b0 + ln], data0=zb[:, b0:b0 + ln], data1=zb[:, b0 - w:b0 - w + ln],
                        initial=0.0, op0=ALU.add, op1=ALU.subtract))(i, zb, b0, ln, w), [ZB[i]], [PA[i]])
                cur, CUR = pa[i], PA[i]
                for (pc, pn, sh) in pieces(p, 0, T, G):
                    stt(z[:, m, pc:pc + pn], cur[:, O + pc + sh:O + pc + sh + pn], 1.0 / w,
                        zb[:, O + pc + sh:O + pc + sh + pn], ALU.mult, ALU.subtract,
                        [CUR, ZB[i]], [Z[m][s] for s in subs_overlapping(pc, pn)])
                if p == 0:
                    a0 = O + HALO
                    tt(tmp15[:, :], cur[:, a0:a0 + G], tab[:, g, :], ALU.mult, [CUR, TAB], [TMP15])
                    tt(z[:, m, HALO:HALO + G], tmp15[:, :], zb[:, a0:a0 + G], ALU.subtract,
                       [TMP15, ZB[i]], [Z[m][s] for s in subs_overlapping(HALO, G)])
            ns.begin()
            for g in range(4):
                ns.zalt = (g == 3)
                for m_ in range(g * PGC, (g + 1) * PGC):
                    chunk_ew(m_)
                view, SR = pw[g]
                for jo in range(PGC):
                    m = g * PGC + jo
                    for s, (c0, n) in enumerate(subs):
                        bank = next_bank()
                        fns = [mm_fn(ps[:, bank, :n], view[:, k, jo * 128:(jo + 1) * 128],
                                     z[:, g * PGC + k, c0:c0 + n], k == 0, k == PGC - 1) for k in range(PGC)]
                        S.op("pe", fns, [SR] + [Z[g * PGC + k][s] for k in range(PGC)], [BANK[bank]])
                        stt(x[:, m, c0:c0 + n], ps[:, bank, :n], gv(5, m), x[:, m, c0:c0 + n],
                            ALU.mult, ALU.add, [BANK[bank], X[m][s], CVEC], [X[m][s]])
                for jo in range(PGC):
                    for s in range(NSUB):
                        ns.tile_done(g * PGC + jo, s)

        def ffn(l, ns, tail_cb=None):
            def w1_group(g, b):
                for q4 in range(HG // 2):
                    col0 = g * HG * 128 + q4 * c.SC
                    view, SR = load_slab(w1[l][:, col0:col0 + c.SC].rearrange("(k p) n -> p k n", p=128),
                                         (KC, c.SC))
                    for j in range(2):
                        hc = 2 * q4 + j
                        for s, (c0, n) in enumerate(subs):
                            bank = next_bank()
                            fns = [mm_fn(ps[:, bank, :n], view[:, k, j * 128:(j + 1) * 128],
                                         z[:, k, c0:c0 + n], k == 0, k == KC - 1) for k in range(KC)]
                            S.op("pe", fns, [SR] + [Z[k][s] for k in range(KC)], [BANK[bank]])
                            ri = state["rt"]
                            state["rt"] = (ri + 1) % NRT
                            S.op("act", act_fn(rtb[ri][:, :n], ps[:, bank, :n], AF.Relu), [BANK[bank]], [RT[ri]])
                            tt(rtb[ri][:, :n], rtb[ri][:, :n], rbuf[:, c0:c0 + n], ALU.mult,
                               [RT[ri], RB[s]], [RT[ri]])
                            tt(scr[:, b * HG + hc, c0:c0 + n], ps[:, bank, :n], rtb[ri][:, :n], ALU.mult,
                               [BANK[bank], RT[ri]], [SCR[b * HG + hc][s]])

            def w2_group(g, b, last=False):
                if last:
                    ns.begin()
                for q in range(D // c.SC2):
                    view, SR = load_slab(
                        w2[l][g * HG * 128:(g + 1) * HG * 128, q * c.SC2:(q + 1) * c.SC2].rearrange(
                            "(k p) n -> p k n", p=128), (HG, c.SC2))
                    for j in range(c.SC2 // 128):
                        m = q * (c.SC2 // 128) + j
                        for s, (c0, n) in enumerate(subs):
                            bank = next_bank()
                            fns = [mm_fn(ps[:, bank, :n], view[:, k, j * 128:(j + 1) * 128],
                                         scr[:, b * HG + k, c0:c0 + n], k == 0, k == HG - 1) for k in range(HG)]
                            S.op("pe", fns, [SR] + [SCR[b * HG + k][s] for k in range(HG)], [BANK[bank]])
                            tt(x[:, m, c0:c0 + n], ps[:, bank, :n], x[:, m, c0:c0 + n], ALU.add,
                               [BANK[bank], X[m][s]], [X[m][s]])
                            if last:
                                ns.tile_done(m, s)

            def w2_last_smajor(g, b):
                nq = D // c.SC2
                assert nq <= c.NB
                sl = [load_slab(w2[l][g * HG * 128:(g + 1) * HG * 128, q * c.SC2:(q + 1) * c.SC2].rearrange(
                    "(k p) n -> p k n", p=128), (HG, c.SC2)) for q in range(nq)]
                ns.begin()
                todo = []
                for s, (c0, n) in enumerate(subs):
                    for q in range(nq):
                        view, SR = sl[q]
                        for j in range(c.SC2 // 128):
                            m = q * (c.SC2 // 128) + j
                            bank = next_bank()
                            fns = [mm_fn(ps[:, bank, :n], view[:, k, j * 128:(j + 1) * 128],
                                         scr[:, b * HG + k, c0:c0 + n], k == 0, k == HG - 1) for k in range(HG)]
                            S.op("pe", fns, [SR] + [SCR[b * HG + k][s] for k in range(HG)], [BANK[bank]])
                            tt(x[:, m, c0:c0 + n], ps[:, bank, :n], x[:, m, c0:c0 + n], ALU.add,
                               [BANK[bank], X[m][s]], [X[m][s]])
                            ns.tile_done(m, s)
                        if q == 0 and todo:
                            ns.flush_sub(todo[0])
                            tail_cb(todo.pop(0))
                    todo.append(s)
                ns.flush()
                for s in todo:
                    tail_cb(s)

            w1_group(0, 0)
            for g in range(1, NG):
                w1_group(g, g % 2)
                w2_group(g - 1, (g - 1) % 2)
            if tail_cb is None:
                w2_group(NG - 1, (NG - 1) % 2, last=True)
            else:
                w2_last_smajor(NG - 1, (NG - 1) % 2)

        store_toks = []

        rtok = sb("rtok", [128, 4], F32)
        RTOK = [Res(f"rtok{i}") for i in range(4)]
        state["rtok"] = 0

        def out_tile(src_fn, src_res_fn, n, r0, rms=None):
            b = state["stage"]
            state["stage"] = (b + 1) % c.NST
            st = stage[b]
            ri = None
            if rms is not None:
                ri = state["rtok"]
                state["rtok"] = (ri + 1) % 4
                bank = next_bank()
                S.op("pe", tr_fn(ps[:n, bank, 0:128], rms[0], ident[:, :]), [IDENT, rms[1]], [BANK[bank]])
                S.op("dve", (lambda n, bank, ri: lambda e: e.reciprocal(out=rtok[:n, ri:ri + 1],
                                                                      in_=ps[:n, bank, 0:1]))(n, bank, ri),
                     [BANK[bank]], [RTOK[ri]])
            for m4 in range(NQ4):
                bank = next_bank()
                fns = [tr_fn(ps[:n, bank, q * 128:(q + 1) * 128], src_fn(4 * m4 + q), ident[:, :])
                       for q in range(4)]
                rr = [IDENT]
                for q in range(4):
                    rr += src_res_fn(4 * m4 + q)
                S.op("pe", fns, rr, [BANK[bank]])
                dst = st[:n, m4 * 512:(m4 + 1) * 512]
                wr = [STAGE[b][m4]] + (STX[b] if m4 == 0 else [])
                if ri is None:
                    copy_op(alt_eng(), dst, ps[:n, bank, :], [BANK[bank]], wr)
                elif alt_eng() == "act":
                    S.op("act", act_fn(dst, ps[:n, bank, :], AF.Copy, scale=rtok[:n, ri:ri + 1]),
                         [BANK[bank], RTOK[ri]], wr)
                else:
                    S.op("dve", (lambda dst, n, bank, ri: lambda e: e.tensor_scalar(
                        out=dst, in0=ps[:n, bank, :], scalar1=rtok[:n, ri:ri + 1], scalar2=None,
                        op0=ALU.mult))(dst, n, bank, ri), [BANK[bank], RTOK[ri]], wr)
            tok = S.dma("sp", (lambda st, n, r0: lambda e: e.dma_start(out=out[r0:r0 + n, :], in_=st[:n, :]))(st, n, r0),
                        stage_sem[b], reads=STAGE[b] + STX[b], writes=())
            store_toks.append(tok)

        def store_sub(p, s):
            if p == 0:
                segs = [(HALO, c.NA, 0)]
            else:
                segs = [(0, c.NBP, c.NA), (c.SAMP0, c.NS, c.NPC)]
            for (cs, cnt, r0) in segs:
                if True:
                    s0, sn = subs[s]
                    lo, hi = max(cs, s0), min(cs + cnt, s0 + sn)
                    for c0 in range(lo, hi, 128):
                        n = min(128, hi - c0)
                        rsx, RSX = rsb[s]
                        out_tile(lambda m, c0=c0, n=n: x[:, m, c0:c0 + n],
                                 lambda m, s=s: [X[m][s]], n, r0 + (c0 - cs),
                                 rms=(rsx[:, c0 - s0:c0 - s0 + n], RSX))

        for p in range(2):
            load_pass(p)
            ns = NormStream(zvec=2)
            mixer_conv(p, ns)
            norm(2, "zr", ns)
            ns = NormStream()
            ffn(0, ns)
            norm(1, "stats", ns)
            if p == 0:
                subs_full = list(subs)
                subs[0] = (HALO, subs[0][1] - HALO)
            ns = NormStream(zvec=3)
            mixer_pool(p, ns)
            norm(3, "zr", ns)
            ns = NormStream(xg=4)
            ffn(1, ns, tail_cb=(lambda ns, p: lambda s: (norm_sub(s, 4, "finalr", ns), store_sub(p, s)))(ns, p))
            if p == 0:
                subs[:] = subs_full
        out_tile(lambda m: st_all[:, m, :], lambda m: [STALL], NSTATE, c.NPC + c.NS)
        S.wait_only("sp", store_toks)

        with nc.Block() as block:
            @block.tensor
            def _(e):
                S.replay("pe", e)

            @block.scalar
            def _(e):
                S.replay("act", e)

            @block.vector
            def _(e):
                S.replay("dve", e)

            @block.gpsimd
            def _(e):
                S.replay("pool", e)

            @block.sync
            def _(e):
                S.replay("sp", e)
    return nc


def host_inputs(cfg, core, x_prompt, x_sample, cache_conv, cache_pool, mix_norm, ffn_norm, conv_w_in,
                conv_w, conv_w_out, pool_w, pool_scale, ffn_w1, ffn_w2, final_norm):
    c = cfg
    T, D, KC = c.T, c.D, c.KC
    s0 = core * c.NPC
    xp = x_prompt[0]
    xin = np.zeros((2 * T + HALO, D), np.float32)
    lo = s0 - HALO
    a = max(lo, 0)
    xin[a - lo:HALO + c.NA] = xp[a:s0 + c.NA]
    xin[T:T + c.NBP] = xp[s0 + c.NA:s0 + c.NPC]
    xin[T + c.SAMP0:T + c.SAMP0 + c.NS] = x_sample[core]
    xin[2 * T:2 * T + CONV_HIST] = cache_conv[0, core]
    xin[2 * T + CONV_HIST:2 * T + HALO] = cache_pool[0, core]
    vecs = [mix_norm[0], mix_norm[1], ffn_norm[0], ffn_norm[1], final_norm, pool_scale[0],
            conv_w[0, 0], conv_w[0, 1], conv_w[0, 2]]
    cvec = np.concatenate([np.ascontiguousarray(v.reshape(KC, 128).T) for v in vecs], axis=1)
    posv = np.broadcast_to((s0 + 1 + np.arange(POOL_HIST)).astype(np.float32)[None, :], (128, POOL_HIST))
    return {
        "xin": xin,
        "cvec": np.ascontiguousarray(cvec, dtype=np.float32),
        "ident": np.eye(128, dtype=np.float32),
        "posv": np.ascontiguousarray(posv),
        "w_in": conv_w_in[0],
        "w_out": conv_w_out[0],
        "pool_w": pool_w[0],
        "w1": ffn_w1,
        "w2": ffn_w2,
    }


_NC_CACHE = {}


def run(cfg, inputs):
    key = (cfg.D, cfg.DFF, cfg.NPC, cfg.NS, cfg.T, cfg.subs[0][1])
    if key not in _NC_CACHE:
        _NC_CACHE[key] = build_program(cfg)
    nc = _NC_CACHE[key]
    inputs = {k: np.ascontiguousarray(np.asarray(v, dtype=np.float32)) for k, v in inputs.items()}
    in_maps = [host_inputs(cfg, core, **inputs) for core in range(cfg.ncores)]
    res = run_bass_kernel_spmd(nc, in_maps, core_ids=list(range(cfg.ncores)))
    outs = [r["out"] for r in res.results]
    c = cfg
    n = c.ncores
    y_prompt = np.concatenate([o[0:c.NPC] for o in outs], axis=0)[None]
    y_sample = np.stack([o[c.NPC:c.NPC + c.NS] for o in outs], axis=0)
    b = c.NPC + c.NS
    conv_p = outs[n - 1][b:b + 2][None, None]
    conv_s = np.stack([o[b + 2:b + 4] for o in outs], axis=0)[None]
    pool_p = outs[n - 1][b + 4:b + 19][None, None]
    pool_s = np.stack([o[b + 19:b + 34] for o in outs], axis=0)[None]
    return (np.ascontiguousarray(y_prompt), np.ascontiguousarray(y_sample), np.ascontiguousarray(conv_p),
            np.ascontiguousarray(pool_p), np.ascontiguousarray(conv_s), np.ascontiguousarray(pool_s))


def kernel(**inputs):
    return run(Cfg(), inputs)
```
